# Optimizing a Trainium2 kernel written in Bass

```python
import math
import jax, jax.numpy as jnp
from jax import lax
import numpy as np

D_MODEL = 1024
BATCH = 2
SEQ = 16384
DEPTH = 2

GRID_W = 64
CTX_LEN = 256
Q_BLOCK = 128
ROPE_THETA = 10000.0
EPS = 1e-6
N_BRANCH = 3

A_HEAD_DIM = 64
A_WIDTH = D_MODEL // 2
A_HEADS = A_WIDTH // A_HEAD_DIM
A_KV_HEADS = A_HEADS // 4
A_GROUP = A_HEADS // A_KV_HEADS
A_KV_WIDTH = A_KV_HEADS * A_HEAD_DIM

B_WIDTH = D_MODEL // 4
B_GROUP_DIM = 64
B_GROUPS = B_WIDTH // B_GROUP_DIM
B_CHUNK = 128

C_WIDTH = D_MODEL // 4
C_QK_DIM = 32
C_V_DIM = 2 * C_QK_DIM
C_HEADS = C_WIDTH // C_V_DIM
C_QK_WIDTH = C_HEADS * 2 * C_QK_DIM

SPLIT_SIZES = (A_WIDTH, C_QK_WIDTH, A_KV_WIDTH, A_KV_WIDTH, C_QK_WIDTH, C_WIDTH,
               A_WIDTH, C_WIDTH, B_WIDTH, B_WIDTH, B_WIDTH, N_BRANCH * D_MODEL)
SPLIT_OFFSETS = tuple(int(o) for o in np.cumsum(SPLIT_SIZES)[:-1])
PROJ_WIDTH = int(sum(SPLIT_SIZES))
KV_START = A_WIDTH + C_QK_WIDTH
KV_END = KV_START + 2 * A_KV_WIDTH + C_QK_WIDTH + C_WIDTH
KV_OFFSETS = (A_KV_WIDTH, 2 * A_KV_WIDTH, 2 * A_KV_WIDTH + C_QK_WIDTH)

kernel_name = 'hybrid_gqa_sgu_diffattn_prefix_dit'


def rms_norm(x, g):
    xf = x.astype(jnp.float32)
    y = xf * lax.rsqrt(jnp.mean(xf * xf, axis=-1, keepdims=True) + EPS)
    return (y * g.astype(jnp.float32)).astype(x.dtype)


def modulate(x, g, shift, scale):
    return rms_norm(x, g) * (1.0 + scale) + shift


def axial_rope_tables(rows, cols, dim):
    quarter = dim // 4
    inv = jnp.power(ROPE_THETA, -jnp.arange(quarter, dtype=jnp.float32) / quarter)
    ang = jnp.concatenate([rows[:, None].astype(jnp.float32) * inv,
                           cols[:, None].astype(jnp.float32) * inv], axis=-1)
    return jnp.cos(ang)[None, :, None, :], jnp.sin(ang)[None, :, None, :]


def apply_rope(x, cos, sin):
    xf = x.astype(jnp.float32).reshape(*x.shape[:-1], x.shape[-1] // 2, 2)
    x0, x1 = xf[..., 0], xf[..., 1]
    out = jnp.stack([x0 * cos - x1 * sin, x0 * sin + x1 * cos], axis=-1)
    return out.reshape(x.shape).astype(x.dtype)


def prep_kv(a_k, a_v, c_k, c_v, gk_a, gk_c, rope_a, rope_c):
    b, t = a_k.shape[:2]
    ka = rms_norm(a_k.reshape(b, t, A_KV_HEADS, A_HEAD_DIM), gk_a)
    kc = rms_norm(c_k.reshape(b, t, 2 * C_HEADS, C_QK_DIM), gk_c)
    if rope_a is not None:
        ka = apply_rope(ka, *rope_a)
        kc = apply_rope(kc, *rope_c)
    return (ka, a_v.reshape(b, t, A_KV_HEADS, A_HEAD_DIM),
            kc.reshape(b, t, C_HEADS, 2, C_QK_DIM), c_v.reshape(b, t, C_HEADS, C_V_DIM))


def prep_q(a_q, c_q, gq_a, gq_c, rope_a, rope_c):
    b, t = a_q.shape[:2]
    qa = rms_norm(a_q.reshape(b, t, A_HEADS, A_HEAD_DIM), gq_a)
    qc = rms_norm(c_q.reshape(b, t, 2 * C_HEADS, C_QK_DIM), gq_c)
    if rope_a is not None:
        qa = apply_rope(qa, *rope_a)
        qc = apply_rope(qc, *rope_c)
    return (qa.reshape(b, t, A_KV_HEADS, A_GROUP, A_HEAD_DIM),
            qc.reshape(b, t, C_HEADS, 2, C_QK_DIM))


def gqa_attend(q, k, v):
    s = jnp.einsum('bqhgd,bkhd->bhgqk', q, k).astype(jnp.float32) * (A_HEAD_DIM ** -0.5)
    p = jax.nn.softmax(s, axis=-1).astype(v.dtype)
    return jnp.einsum('bhgqk,bkhd->bqhgd', p, v)


def diff_attend(q, k, v, lam):
    s = jnp.einsum('bqhmd,bkhmd->bhmqk', q, k).astype(jnp.float32) * (C_QK_DIM ** -0.5)
    p = jax.nn.softmax(s, axis=-1)
    p = p[:, :, 0] - lam * p[:, :, 1]
    return jnp.einsum('bhqk,bkhe->bqhe', p.astype(v.dtype), v)


def diff_post(o, g_subln, lam_init):
    b, t = o.shape[:2]
    return (rms_norm(o, g_subln) * (1.0 - lam_init)).reshape(b, t, C_WIDTH)


def sweep_query_blocks(fn, q):
    b, t = q.shape[:2]
    nb = t // Q_BLOCK
    qb = jnp.moveaxis(q.reshape(b, nb, Q_BLOCK, *q.shape[2:]), 1, 0)
    ob = lax.map(fn, qb)
    return jnp.moveaxis(ob, 0, 1).reshape(b, t, *ob.shape[3:])


def spatial_gate(u, v, g_sgu, w_sp, b_sp):
    b, t, _ = v.shape
    vn = rms_norm(v.reshape(b, t, B_GROUPS, B_GROUP_DIM), g_sgu)
    vc = vn.reshape(b, t // B_CHUNK, B_CHUNK, B_GROUPS, B_GROUP_DIM)
    mixed = jnp.einsum('gpq,bnqgc->bnpgc', w_sp, vc) + b_sp.T[:, :, None]
    return u * mixed.reshape(b, t, B_WIDTH)


def merge_branches(o_a, o_b, o_c, z_a, z_b, z_c, gates, w_pa, w_pb, w_pc, w_out):
    y_a = (o_a * jax.nn.silu(z_a)) @ w_pa
    y_b = (o_b * jax.nn.silu(z_b)) @ w_pb
    y_c = (o_c * jax.nn.silu(z_c)) @ w_pc
    g = jax.nn.sigmoid(gates.reshape(*gates.shape[:-1], N_BRANCH, D_MODEL))
    m = g[..., 0, :] * y_a + g[..., 1, :] * y_b + g[..., 2, :] * y_c
    return m @ w_out


def trunk_layer(x, ctx, c, c_ctx, w_mod, b_mod, g_norm, w_in, gq_a, gk_a, gq_c, gk_c,
                g_sgu, w_sp, b_sp, lam_c, g_subln, w_pa, w_pb, w_pc, w_out,
                lam_init, update_ctx, rope_a, rope_c):
    b, t, _ = x.shape
    mod = jax.nn.silu(c) @ w_mod + b_mod
    shift, scale, gate = jnp.split(mod[:, None, :], 3, axis=-1)
    mod_ctx = jax.nn.silu(c_ctx) @ w_mod + b_mod
    shift_c, scale_c, gate_c = jnp.split(mod_ctx, 3)

    hn = modulate(x, g_norm, shift, scale)
    (a_q, c_q, a_k, a_v, c_k, c_v, z_a, z_c, b_u, b_v, b_z, gates) = jnp.split(
        hn @ w_in, SPLIT_OFFSETS, axis=-1)

    hn_ctx = modulate(ctx, g_norm, shift_c, scale_c)
    if update_ctx:
        parts_ctx = jnp.split(hn_ctx @ w_in, SPLIT_OFFSETS, axis=-1)
        kv_ctx_parts = parts_ctx[2:6]
    else:
        kv_ctx_parts = jnp.split(hn_ctx @ w_in[:, KV_START:KV_END], KV_OFFSETS, axis=-1)
    ka_x, va_x, kc_x, vc_x = prep_kv(*kv_ctx_parts, gk_a, gk_c, None, None)
    ka_l, va_l, kc_l, vc_l = prep_kv(a_k, a_v, c_k, c_v, gk_a, gk_c, rope_a, rope_c)
    ka = jnp.concatenate([ka_x, ka_l], axis=1)
    va = jnp.concatenate([va_x, va_l], axis=1)
    kc = jnp.concatenate([kc_x, kc_l], axis=1)
    vc = jnp.concatenate([vc_x, vc_l], axis=1)
    qa, qc = prep_q(a_q, c_q, gq_a, gq_c, rope_a, rope_c)

    lam_f = lam_c.astype(jnp.float32)
    lam = (jnp.exp(jnp.sum(lam_f[0] * lam_f[1])) - jnp.exp(jnp.sum(lam_f[2] * lam_f[3]))
           + lam_init)

    o_a = sweep_query_blocks(lambda qb: gqa_attend(qb, ka, va), qa).reshape(b, t, A_WIDTH)
    o_c = diff_post(sweep_query_blocks(lambda qb: diff_attend(qb, kc, vc, lam), qc),
                    g_subln, lam_init)
    o_b = spatial_gate(b_u, b_v, g_sgu, w_sp, b_sp)
    x_new = x + gate * merge_branches(o_a, o_b, o_c, z_a, b_z, z_c, gates,
                                      w_pa, w_pb, w_pc, w_out)

    if update_ctx:
        qa_x, qc_x = prep_q(parts_ctx[0], parts_ctx[1], gq_a, gq_c, None, None)
        oa_x = gqa_attend(qa_x, ka_x, va_x).reshape(b, ctx.shape[1], A_WIDTH)
        oc_x = diff_post(diff_attend(qc_x, kc_x, vc_x, lam), g_subln, lam_init)
        ob_x = spatial_gate(parts_ctx[8], parts_ctx[9], g_sgu, w_sp, b_sp)
        ctx = ctx + gate_c * merge_branches(oa_x, ob_x, oc_x, parts_ctx[6], parts_ctx[10],
                                            parts_ctx[7], parts_ctx[11],
                                            w_pa, w_pb, w_pc, w_out)
    return x_new, ctx


def setup_inputs(seed: int = 0) -> dict:
    key = jax.random.key(seed)
    ks = jax.random.split(key, 24)
    f32 = jnp.float32
    nrm = lambda k, shape, s: jax.random.normal(k, shape, f32) * s
    gain = lambda k, shape: 1.0 + 0.02 * jax.random.normal(k, shape, f32)
    return {
        'x': nrm(ks[0], (BATCH, SEQ, D_MODEL), 1.0),
        'c': nrm(ks[1], (BATCH, D_MODEL), 1.0),
        'ctx': nrm(ks[2], (BATCH, CTX_LEN, D_MODEL), 1.0),
        'c_ctx': nrm(ks[3], (D_MODEL,), 1.0),
        'w_mod': nrm(ks[4], (DEPTH, D_MODEL, 3 * D_MODEL), 0.5 * D_MODEL ** -0.5),
        'b_mod': nrm(ks[5], (DEPTH, 3 * D_MODEL), 0.02),
        'g_norm': gain(ks[6], (DEPTH, D_MODEL)),
        'w_in': nrm(ks[7], (DEPTH, D_MODEL, PROJ_WIDTH), D_MODEL ** -0.5),
        'gq_a': gain(ks[8], (DEPTH, A_HEAD_DIM)),
        'gk_a': gain(ks[9], (DEPTH, A_HEAD_DIM)),
        'gq_c': gain(ks[10], (DEPTH, C_QK_DIM)),
        'gk_c': gain(ks[11], (DEPTH, C_QK_DIM)),
        'g_sgu': gain(ks[12], (DEPTH, B_GROUPS, B_GROUP_DIM)),
        'w_sp': nrm(ks[13], (DEPTH, B_GROUPS, B_CHUNK, B_CHUNK), B_CHUNK ** -0.5),
        'b_sp': nrm(ks[14], (DEPTH, B_GROUPS, B_CHUNK), 0.02),
        'lam_c': nrm(ks[15], (DEPTH, 4, C_QK_DIM), 0.1),
        'g_subln': gain(ks[16], (DEPTH, C_V_DIM)),
        'w_pa': nrm(ks[17], (DEPTH, A_WIDTH, D_MODEL), A_WIDTH ** -0.5),
        'w_pb': nrm(ks[18], (DEPTH, B_WIDTH, D_MODEL), B_WIDTH ** -0.5),
        'w_pc': nrm(ks[19], (DEPTH, C_WIDTH, D_MODEL), C_WIDTH ** -0.5),
        'w_out': nrm(ks[20], (DEPTH, D_MODEL, D_MODEL), D_MODEL ** -0.5),
    }


def reference(x, c, ctx, c_ctx, w_mod, b_mod, g_norm, w_in, gq_a, gk_a, gq_c, gk_c,
              g_sgu, w_sp, b_sp, lam_c, g_subln, w_pa, w_pb, w_pc, w_out):
    n_tok = x.shape[1]
    ROWS = n_tok // GRID_W
    rows = jnp.repeat(jnp.arange(ROWS), GRID_W)
    cols = jnp.tile(jnp.arange(GRID_W), ROWS)
    rope_a = axial_rope_tables(rows, cols, A_HEAD_DIM)
    rope_c = axial_rope_tables(rows, cols, C_QK_DIM)
    for l in range(DEPTH):
        lam_init = 0.8 - 0.6 * math.exp(-0.3 * l)
        x, ctx = trunk_layer(x, ctx, c, c_ctx, w_mod[l], b_mod[l], g_norm[l], w_in[l],
                             gq_a[l], gk_a[l], gq_c[l], gk_c[l], g_sgu[l], w_sp[l], b_sp[l],
                             lam_c[l], g_subln[l], w_pa[l], w_pb[l], w_pc[l], w_out[l],
                             lam_init, l < DEPTH - 1, rope_a, rope_c)
    return x
```

```python
import contextlib
import math
import numpy as np
import ml_dtypes
import concourse.bass as bass
import concourse.mybir as mybir
from concourse.bass_utils import run_bass_kernel_spmd

F32 = mybir.dt.float32
BF16 = mybir.dt.bfloat16
ALU = mybir.AluOpType
AF = mybir.ActivationFunctionType
AX = mybir.AxisListType

NCORES = 8
D = 1024
SEQ = 16384
OWN = 4096
CTX = 256
NT_OWN = 32
NT = 34
EPS = 1e-6
NKT = 130

C_KV = 0
C_QA = 768
C_QC = 1280
C_M = 1536
C_BV = 5888
S_GQA, S_GKA, S_GQC, S_GKC, S_SGU, S_BSP, NS = 0, 64, 128, 160, 192, 448, 960

COMPUTE = ("tensor", "vector", "scalar", "gpsimd")
QUEUES = ("sync", "tensor", "vector", "scalar", "gpsimd")


class Res:
    __slots__ = ("name", "last_w", "readers", "excl")

    def __init__(self, name="", excl=False):
        self.name = name
        self.last_w = None
        self.readers = {}
        self.excl = excl


class Entry:
    __slots__ = ("eng", "fn", "waits", "signal", "val", "sem", "inc", "kind")

    def __init__(self, eng, fn, kind):
        self.eng = eng
        self.fn = fn
        self.waits = []
        self.signal = False
        self.val = None
        self.sem = None
        self.inc = 0
        self.kind = kind


class Prog:
    def __init__(self, nc, stack, same_engine_sync=True):
        self.nc = nc
        self.stack = stack
        self.q = {e: [] for e in QUEUES}
        self.esem = {e: stack.enter_context(nc.semaphore("es_" + e)) for e in COMPUTE}
        self.same = same_engine_sync
        self.nsem = 0
        self.final = []

    def new_sem(self, name=None):
        self.nsem += 1
        return [self.stack.enter_context(self.nc.semaphore(name or f"ds{self.nsem}")), 0]

    def _deps(self, e, reads, writes):
        ex = [r for r in reads if r.excl]
        if ex:
            reads = [r for r in reads if not r.excl]
            writes = list(writes) + ex
        deps = []
        for r in reads:
            if r.last_w is not None:
                deps.append(r.last_w)
        for w in writes:
            if w.last_w is not None:
                deps.append(w.last_w)
            deps.extend(w.readers.values())
        seen = set()
        for d in deps:
            if d is e or id(d) in seen:
                continue
            seen.add(id(d))
            if d.kind == "c":
                if d.eng == e.eng and e.kind == "c" and (d.eng == "tensor" or not self.same):
                    continue
                d.signal = True
            e.waits.append(d)
        for r in reads:
            key = e.eng if e.kind == "c" else id(e)
            r.readers[key] = e
        for w in writes:
            w.last_w = e
            w.readers = {}

    def op(self, eng, fn, reads=(), writes=()):
        e = Entry(eng, fn, "c")
        self._deps(e, reads, writes)
        self.q[eng].append(e)
        return e

    def dma(self, eng, fn, sem, reads=(), writes=(), inc=16):
        e = Entry(eng, fn, "d")
        self._deps(e, reads, writes)
        sem[1] += inc
        e.sem = sem[0]
        e.inc = inc
        e.val = sem[1]
        self.q[eng].append(e)
        return e

    def wait_at_end(self, e):
        if e.kind == "c":
            e.signal = True
        self.final.append(e)

    def emit(self):
        for eng in COMPUTE:
            c = 0
            for e in self.q[eng]:
                if e.kind == "c" and e.signal:
                    c += 1
                    e.val = c
        stats = {}
        with self.nc.Block() as block:
            for eng in QUEUES:
                entries = self.q[eng]
                if not entries and eng != "sync":
                    continue
                stats[eng] = len(entries)

                def body(engine, entries=entries, eng=eng):
                    waited = {}

                    def wait(d):
                        sem = self.esem[d.eng] if d.kind == "c" else d.sem
                        k = id(sem)
                        if waited.get(k, 0) >= d.val:
                            return
                        waited[k] = d.val
                        engine.wait_ge(sem, d.val)

                    for e in entries:
                        for d in e.waits:
                            wait(d)
                        ins = e.fn(engine)
                        if e.kind == "c":
                            if e.signal:
                                ins.then_inc(self.esem[eng], 1)
                        else:
                            ins.then_inc(e.sem, e.inc)
                    if eng == "sync":
                        for d in self.final:
                            wait(d)

                getattr(block, eng)(body)
        return stats


class _Stop(Exception):
    pass


def build_program(n_layers=2, stop=None):
    nc = bass.Bass("TRN2", target_bir_lowering=False)
    dt = lambda name, shape, dtp, kind=None: (nc.dram_tensor(name, shape, dtp, kind=kind) if kind
                                              else nc.dram_tensor(name, shape, dtp))
    x_in = dt("x", [OWN, D], F32, "ExternalInput").ap()
    ctx_in = dt("ctx", [CTX, D], F32, "ExternalInput").ap()
    cc_in = dt("cc", [128, 16], F32, "ExternalInput").ap()
    wmod_in = dt("w_mod", [2, D, 3 * D], F32, "ExternalInput").ap()
    bmT_in = dt("bmT", [2, 128, 24], F32, "ExternalInput").ap()
    bmod_in = dt("b_mod", [2, 3 * D], F32, "ExternalInput").ap()
    gn_in = dt("gn", [2, 128, 8], F32, "ExternalInput").ap()
    win_in = dt("w_in", [2, D, 6144], F32, "ExternalInput").ap()
    small_in = dt("small", [2, NS], F32, "ExternalInput").ap()
    gsub_in = dt("gsub", [128, 2], F32, "ExternalInput").ap()
    lam_in = dt("lam", [2, 128], F32, "ExternalInput").ap()
    wsp_in = dt("w_spT", [2, 128, 4, 128], F32, "ExternalInput").ap()
    wpa_in = dt("w_pa", [2, 512, D], F32, "ExternalInput").ap()
    wpb_in = dt("w_pb", [2, 256, D], F32, "ExternalInput").ap()
    wpc_in = dt("w_pc", [2, 256, D], F32, "ExternalInput").ap()
    wout_in = dt("w_out", [2, D, D], F32, "ExternalInput").ap()
    rope_in = dt("rope", [NT_OWN, 128, 96], F32, "ExternalInput").ap()
    ident_in = dt("ident", [128, 128], BF16, "ExternalInput").ap()
    sel_in = dt("sel", [2, 2, 128], F32, "ExternalInput").ap()
    out_d = dt("out", [OWN, D], F32, "ExternalOutput").ap()

    x1_d = dt("x1", [OWN, D], F32).ap()
    ctx1_d = dt("ctx1", [CTX, D], F32).ap()
    hn_d = dt("hn_scr", [NT, 128, 8, 128], BF16).ap()
    o_d = dt("o_scr", [64, 12, OWN + CTX], BF16).ap()
    lock_t = [dt(f"lock{c}", [64, OWN], BF16) for c in range(6)]
    gk_t = [dt(f"gk{c}", [256, OWN], BF16) for c in range(6)]
    locv_t = [dt(f"locv{c}", [48, OWN], BF16) for c in range(8)]
    gv_t = [dt(f"gv{c}", [192, OWN], BF16) for c in range(8)]
    kx_d = dt("kx", [384, CTX], BF16).ap()
    vx_d = dt("vx", [CTX, 384], BF16).ap()

    def vview(ap2d):
        return ap2d.rearrange("r c -> (r c)").rearrange("(t f) -> t f", f=384)

    with contextlib.ExitStack() as st:
        P = Prog(nc, st)
        sb = lambda n, s, d: st.enter_context(nc.sbuf_tensor("sb_" + n, s, d))
        pst = lambda n, s, d: st.enter_context(nc.psum_tensor(n, s, d))

        ARENA_N = 55296
        arena = sb("arena", [128, ARENA_N], BF16)
        KT = arena[:, 0:16640]
        VA = arena[:, 16640:16640 + NKT * 256].rearrange("p (k h c) -> p k h c", k=NKT, h=2)
        wKV = arena[:, 0:6144].rearrange("p (f n) -> p f n", f=8)
        wM = arena[:, 0:8 * 4352].rearrange("p (f n) -> p f n", f=8)
        wBV = arena[:, 34816:34816 + 2048].rearrange("p (f n) -> p f n", f=8)
        wOUT = arena[:, 36864:36864 + 8192].rearrange("p (f n) -> p f n", f=8)
        wPA = arena[:, 45056:45056 + 4096].rearrange("p (c n) -> p c n", c=4)
        wPC = arena[:, 49152:49152 + 2048].rearrange("p (c n) -> p c n", c=2)
        wPB = arena[0:64, 51200:51200 + 4096].rearrange("p (c n) -> p c n", c=4)
        rKT, rVA, rWM, rWKV = Res("KT"), Res("VA"), Res("wM"), Res("wKV")
        ARENA_ALL = [rKT, rVA, rWM, rWKV]
        ident = sb("ident", [128, 128], BF16)
        ones64 = sb("ones64", [64, 64], F32)
        sel = sb("sel", [2, 2, 128], F32)
        cc = sb("cc", [128, 16], F32)
        silc = sb("silc", [128, 16], F32)
        small = sb("small", [128, 2 * NS], F32)
        modT = sb("modT", [128, 2, 24, 2], F32)
        bmT = sb("bmT", [128, 2, 24], F32)
        gn = sb("gn", [128, 2, 8], F32)
        gmod = sb("gmod", [128, 2, 2, 8], F32)
        shf = sb("shf", [128, 2, 2, 8], F32)
        gate_bc = sb("gate_bc", [128, 2, 1024], F32)
        lamt = sb("lamt", [128, 2, 128], F32)
        lamw = sb("lamw", [128, 2, 8], F32)
        gsub = sb("gsub", [128, 2], F32)
        ss = sb("ss", [128, 16], F32)
        ssx = sb("ssx", [128, 2], F32)
        rconst, rmod, rgate, rss, rssx = Res("const"), Res("mod"), Res("gate_bc"), Res("ss"), Res("ssx")

        SCR_BYTES = 57344
        scr = sb("scr", [128, SCR_BYTES // 2], BF16)

        class Carver:
            def __init__(self):
                self.off = 0

            def take(self, nfree, dtype, parts=128):
                n2 = nfree * (2 if dtype == F32 else 1)
                n2 = (n2 + 15) // 16 * 16
                assert self.off + n2 <= SCR_BYTES // 2, (self.off, n2)
                ap = scr[0:parts, self.off:self.off + n2]
                self.off += n2
                if dtype == F32:
                    ap = ap.bitcast(F32)
                return ap[:, 0:nfree]

        class Rot:
            def __init__(self, name, aps):
                self.slots = [(a, Res(f"{name}{i}"), P.new_sem()) for i, a in enumerate(aps)]
                self.i = 0

            def next(self):
                s_ = self.slots[self.i % len(self.slots)]
                self.i += 1
                return s_

            def res(self):
                return [s_[1] for s_ in self.slots]

        cA = Carver()
        X_A = Rot("xA", [cA.take(D, F32) for _ in range(2)])
        XN = Rot("xn", [cA.take(D, BF16)])
        HT = Rot("hT", [cA.take(1024, BF16).rearrange("p (f t) -> p f t", f=8) for _ in range(2)])
        STG_A = Rot("stgA", [cA.take(1024, F32) for _ in range(2)])
        RP_A = Rot("rpA", [cA.take(96, F32) for _ in range(2)])
        KTS = Rot("kts", [cA.take(384, BF16).rearrange("p (j t) -> p j t", j=3) for _ in range(2)])
        VB = Rot("vb", [cA.take(384, BF16) for _ in range(2)])
        sq_A, kn_A, kr_A = cA.take(1024, F32), cA.take(512, F32), cA.take(512, BF16)
        rt1_A, rt2_A = cA.take(256, F32), cA.take(256, F32)
        res32_A = cA.take(1024, F32)
        cT = Carver()
        HN_T = Rot("hnT", [cT.take(4096, BF16).rearrange("p (i f t) -> p i f t", i=4, f=8)])
        STG_T = Rot("stgT", [cT.take(1024, F32) for _ in range(2)])
        RP_T = Rot("rpT", [cT.take(96, F32) for _ in range(2)])
        wS = cT.take(4096, BF16).rearrange("p (f n) -> p f n", f=8)
        rWS = Res("wS")
        sq_T, kn_T, kr_T = cT.take(512, F32), cT.take(512, F32), cT.take(512, BF16)
        rt1_T, rt2_T = cT.take(256, F32), cT.take(256, F32)
        PT = Rot("pt", [cT.take(1024, BF16) for _ in range(3)])
        QT = Rot("qT", [cT.take(2048, BF16).rearrange("p (j t) -> p j t", j=4)])
        ONB = Rot("onb", [cT.take(512, BF16) for _ in range(2)])
        R = cT.take(512, F32)
        on32 = cT.take(1024, F32).rearrange("p (m t) -> p m t", m=2)
        dif, dsq = cT.take(512, F32), cT.take(512, F32)
        rR, ron, rdif, rdsq = Res("R"), [Res("on0"), Res("on1")], Res("dif"), Res("dsq")
        cM = Carver()
        NM = 256
        HN_M = Rot("hnM", [cM.take(2048, BF16).rearrange("p (i f t) -> p i f t", i=2, f=8) for _ in range(2)])
        STG_M = Rot("stgM", [cM.take(1024, F32) for _ in range(2)])
        X_M = Rot("xM", [cM.take(D, F32) for _ in range(2)])
        OG = Rot("og", [cM.take(6 * NM, BF16).rearrange("p (c t) -> p c t", c=6)])
        sq_M, kn_M = cM.take(256, F32), cM.take(256, F32)
        G_AC = cM.take(6 * NM, BF16).rearrange("p (c t) -> p c t", c=6)
        G_B = cM.take(4 * NM, BF16).rearrange("p (c t) -> p c t", c=4)
        mT = cM.take(8 * NM, BF16).rearrange("p (c t) -> p c t", c=8)
        mixsb = cM.take(4 * NM, F32).rearrange("p (c t) -> p c t", c=4)
        sig = cM.take(3 * NM, F32).rearrange("p (c t) -> p c t", c=3)
        ebuf = cM.take(2 * NM, F32).rearrange("p (c t) -> p c t", c=2)
        tbuf = cM.take(2 * NM, F32).rearrange("p (c t) -> p c t", c=2)
        res32_M = cM.take(1024, F32)
        macc = cM.take(NM, F32)
        vn = cM.take(256, BF16)
        wsp = cM.take(512, BF16).rearrange("p (g q) -> p g q", g=4)
        rG, rmT, rmix, rvn, rwsp, rmacc, rres = (Res("G"), Res("mT"), Res("mix"), Res("vn"), Res("wsp"),
                                                 Res("macc"), Res("res32"))
        reb, rtb, rsig = [Res("eb0"), Res("eb1")], [Res("tb0"), Res("tb1")], [Res("sg0"), Res("sg1"), Res("sg2")]
        rsq, rkn, rkr, rrt = Res("sq"), Res("kn"), Res("kr"), Res("rt")
        SCR_RES = ([rsq, rkn, rkr, rrt, rres, rWS, rR, rdif, rdsq, rG, rmT, rmix, rvn, rwsp, rmacc] + ron + reb + rtb
                   + rsig)
        for rot in (X_A, XN, HT, STG_A, RP_A, KTS, VB, HN_T, STG_T, RP_T, PT, QT, ONB, HN_M, STG_M, X_M, OG):
            SCR_RES += rot.res()
        fence_t = sb("fence_t", [128, 8], F32)
        rfence = Res("fence")

        def fence():
            P.op("vector", lambda e: e.memset(fence_t[:], 0.0), (), SCR_RES + [rfence])

        cur = {}

        def set_phase(ph):
            fence()
            if ph == "A":
                cur.update(X=X_A, STG=STG_A, RP=RP_A, sq=sq_A, kn=kn_A, kr=kr_A, rt1=rt1_A, rt2=rt2_A, res32=res32_A)
            elif ph == "T":
                cur.update(HN=HN_T, STG=STG_T, RP=RP_T, sq=sq_T, kn=kn_T, kr=kr_T, rt1=rt1_T, rt2=rt2_T)
            else:
                cur.update(HN=HN_M, STG=STG_M, X=X_M, sq=sq_M, kn=kn_M, res32=res32_M)

        set_phase("A")

        S = [pst("S0", [128, 1024], F32), pst("S1", [128, 1024], F32)]
        rS = [Res("S0", True), Res("S1", True)]
        ACC = [pst("acc0", [128, 512], F32), pst("acc1", [128, 512], F32)]
        rACC = [Res("acc0", True), Res("acc1", True)]
        U = [pst("U0", [128, 512], F32), pst("U1", [128, 512], F32)]
        rU = [Res("U0", True), Res("U1", True)]
        Ubf = [u[:].bitcast(BF16) for u in U]

        def mm(out, lhsT, rhs, start, stop, r, w, tp=None):
            if tp is None:
                return P.op("tensor", lambda e: e.matmul(out, lhsT=lhsT, rhs=rhs, start=start, stop=stop), r, w)
            return P.op("tensor", lambda e: e.matmul(out, lhsT=lhsT, rhs=rhs, start=start, stop=stop,
                                                     tile_position=tp), r, w)

        def tr(out, in_, r, w):
            return P.op("tensor", lambda e: e.transpose(out, in_, ident[:]), list(r) + [rconst], w)

        def act(out, in_, func, r, w, scale=1.0, bias=0.0):
            return P.op("scalar", lambda e: e.activation(out=out, in_=in_, func=func, bias=bias, scale=scale), r, w)

        def tt(out, in0, in1, op, r, w, eng="vector"):
            return P.op(eng, lambda e: e.tensor_tensor(out=out, in0=in0, in1=in1, op=op), r, w)

        def ts(out, in0, s1, op0, r, w, s2=None, op1=None, eng="vector"):
            if op1 is None:
                return P.op(eng, lambda e: e.tensor_scalar(out=out, in0=in0, scalar1=s1, scalar2=None, op0=op0), r, w)
            return P.op(eng, lambda e: e.tensor_scalar(out=out, in0=in0, scalar1=s1, scalar2=s2, op0=op0, op1=op1), r, w)

        def stt(out, in0, scalar, in1, op0, op1, r, w):
            return P.op("vector", lambda e: e.scalar_tensor_tensor(out=out, in0=in0, scalar=scalar, in1=in1,
                                                                   op0=op0, op1=op1), r, w)

        def cp(out, in_, r, w, eng="vector"):
            return P.op(eng, lambda e: e.tensor_copy(out=out, in_=in_), r, w)

        def red(out, in_, r, w):
            return P.op("vector", lambda e: e.tensor_reduce(out=out, in_=in_, axis=AX.X, op=ALU.add), r, w)

        def rcp(out, in_, r, w):
            return P.op("vector", lambda e: e.reciprocal(out=out, in_=in_), r, w)

        def dma(q, out, in_, sem, r, w):
            return P.dma(q, lambda e: e.dma_start(out=out, in_=in_), sem, r, w)

        def mset(ap, val, w):
            return P.op("vector", lambda e: e.memset(ap, val), (), w)

        def rstd_inplace(ap, d, r):
            act(ap, ap, AF.Ln, [r], [r], scale=1.0 / d, bias=EPS)
            act(ap, ap, AF.Exp, [r], [r], scale=-0.5)


        dma("sync", ident[:], ident_in, P.new_sem(), [], [rconst])
        dma("sync", sel[:], sel_in, P.new_sem(), [], [rconst])
        dma("sync", cc[:], cc_in, P.new_sem(), [], [rconst])
        dma("sync", small[:], small_in.rearrange("l n -> (l n)").partition_broadcast(128), P.new_sem(), [], [rconst])
        dma("sync", bmT[:], bmT_in.rearrange("l p k -> p l k"), P.new_sem(), [], [rconst])
        dma("sync", gn[:], gn_in.rearrange("l p k -> p l k"), P.new_sem(), [], [rconst])
        dma("sync", lamt[:], lam_in.rearrange("l n -> (l n)").partition_broadcast(128), P.new_sem(), [], [rconst])
        dma("sync", gsub[:], gsub_in, P.new_sem(), [], [rconst])
        mset(ones64[:], 1.0, [rconst])

        act(silc[:], cc[:], AF.Exp, [rconst], [rmod], scale=-1.0)
        ts(silc[:], silc[:], 1.0, ALU.add, [rmod], [rmod])
        rcp(silc[:], silc[:], [rmod], [rmod])
        tt(silc[:], silc[:], cc[:], ALU.mult, [rmod, rconst], [rmod])

        def mod_pieces(l, n_lo, n_hi, feat, rows):
            for pc in range(n_lo * 2, n_hi * 2):
                stg, rstg, sstg = cur["STG"].next()
                sv = stg[:, 0:1024].rearrange("p (k n) -> p k n", k=8)
                c0 = pc * 128
                dma("sync", sv, wmod_in[l, :, c0:c0 + 128].rearrange("(k p) n -> p k n", p=128), sstg, [], [rstg])
                if feat:
                    for kc in range(8):
                        mm(U[0][:, pc * 2:pc * 2 + 2], sv[:, kc, :], silc[:, kc * 2:kc * 2 + 2], kc == 0, kc == 7,
                           [rstg, rmod], [rU[0]])
                if rows:
                    r0 = c0 - 2048
                    for kc in range(8):
                        mm(S[0][0:2, r0:r0 + 128], silc[:, kc * 2:kc * 2 + 2], sv[:, kc, :], kc == 0, kc == 7,
                           [rstg, rmod], [rS[0]])

        for l in range(n_layers):
            mod_pieces(l, 0, 12, True, False)
            tt(modT[:, l, :, :], U[0][:, 0:48].rearrange("p (k s) -> p k s", s=2),
               bmT[:, l, :].unsqueeze(2).to_broadcast([128, 24, 2]), ALU.add, [rU[0], rconst], [rmod])
            for s_ in range(2):
                ts(gmod[:, l, s_, :], modT[:, l, 8:16, s_], 1.0, ALU.add, [rmod], [rmod])
                tt(gmod[:, l, s_, :], gmod[:, l, s_, :], gn[:, l, :], ALU.mult, [rmod, rconst], [rmod])
                cp(shf[:, l, s_, :], modT[:, l, 0:8, s_], [rmod], [rmod])
            lam_init = 0.8 - 0.6 * math.exp(-0.3 * l)
            lv = lamt[:, l, :].rearrange("p (a b d) -> p a b d", a=2, b=2)
            sqv = cur["sq"][:, 0:64].rearrange("p (a d) -> p a d", a=2)
            tt(sqv, lv[:, :, 0, :], lv[:, :, 1, :], ALU.mult, [rconst], [rsq])
            red(lamw[:, l, 0:2], sqv, [rsq], [rmod])
            act(lamw[:, l, 0:2], lamw[:, l, 0:2], AF.Exp, [rmod], [rmod])
            tt(lamw[:, l, 2:3], lamw[:, l, 0:1], lamw[:, l, 1:2], ALU.subtract, [rmod], [rmod])
            ts(lamw[:, l, 3:4], lamw[:, l, 2:3], -1.0, ALU.mult, [rmod], [rmod], s2=-lam_init, op1=ALU.add)
            ts(lamw[:, l, 4:5], gsub[:, l:l + 1], 1.0 - lam_init, ALU.mult, [rconst], [rmod])

        def load_cast(dst3, src2, ncols, wres, nparts=128, kdim=8, extra_r=()):
            step = 1024 // kdim
            for c0 in range(0, ncols, step):
                cw = min(step, ncols - c0)
                stg, rstg, sstg = cur["STG"].next()
                sv = stg[0:nparts, 0:kdim * cw].rearrange("p (k n) -> p k n", k=kdim)
                dma("sync", sv, src2[:, c0:c0 + cw].rearrange("(k p) n -> p k n", p=nparts), sstg, [], [rstg])
                cp(dst3[:, :, c0:c0 + cw], sv, [rstg] + list(extra_r), wres)

        def norm_rope(ps_ap, width, groups, r_ps, rp, rrp, rope):
            sq, kn, kr, rt1, rt2 = cur["sq"], cur["kn"], cur["kr"], cur["rt1"], cur["rt2"]
            act(sq[:, 0:width], ps_ap, AF.Square, [r_ps], [rsq])
            so = 0
            for (c0, H, d, goff, ro) in groups:
                red(ss[:, so:so + H], sq[:, c0:c0 + H * d].rearrange("p (h d) -> p h d", h=H), [rsq], [rss])
                act(ss[:, so:so + H], ss[:, so:so + H], AF.Ln, [rss], [rss], scale=1.0 / d, bias=EPS)
                so += H
            act(ss[:, 0:so], ss[:, 0:so], AF.Exp, [rss], [rss], scale=-0.5)
            so = 0
            for (c0, H, d, goff, ro) in groups:
                v = kn[:, c0:c0 + H * d].rearrange("p (h d) -> p h d", h=H)
                tt(v, ps_ap[:, c0:c0 + H * d].rearrange("p (h d) -> p h d", h=H),
                   ss[:, so:so + H].unsqueeze(2).to_broadcast([128, H, d]), ALU.mult, [r_ps, rss], [rkn])
                tt(v, v, small[:, goff:goff + d].unsqueeze(1).to_broadcast([128, H, d]), ALU.mult,
                   [rkn, rconst], [rkn])
                so += H
            if not rope:
                cp(kr[:, 0:width], kn[:, 0:width], [rkn], [rkr])
                return
            for (c0, H, d, goff, ro) in groups:
                hd = d // 2
                v = kn[:, c0:c0 + H * d].rearrange("p (h i two) -> p h i two", h=H, two=2)
                o = kr[:, c0:c0 + H * d].rearrange("p (h i two) -> p h i two", h=H, two=2)
                x0, x1 = v[:, :, :, 0], v[:, :, :, 1]
                cos = rp[:, ro:ro + hd].unsqueeze(1).to_broadcast([128, H, hd])
                sin = rp[:, ro + hd:ro + 2 * hd].unsqueeze(1).to_broadcast([128, H, hd])
                a1 = rt1[:, 0:H * hd].rearrange("p (h i) -> p h i", h=H)
                a2 = rt2[:, 0:H * hd].rearrange("p (h i) -> p h i", h=H)
                tt(a1, x0, cos, ALU.mult, [rkn, rrp], [rrt])
                tt(a2, x1, sin, ALU.mult, [rkn, rrp], [rrt])
                tt(o[:, :, :, 0], a1, a2, ALU.subtract, [rrt], [rkr])
                tt(a1, x0, sin, ALU.mult, [rkn, rrp], [rrt])
                tt(a2, x1, cos, ALU.mult, [rkn, rrp], [rrt])
                tt(o[:, :, :, 1], a1, a2, ALU.add, [rrt], [rkr])

        r_hn = [Res(f"hn{i}") for i in range(NT)]
        r_loc = [[Res(f"loc{i}_{k}") for k in range(7)] for i in range(NT_OWN)]
        r_g = Res("kv_g")
        r_kx = [[Res(f"kx{i}_{k}") for k in range(3)] for i in range(2)]
        r_kx_all = [r for rr in r_kx for r in rr]
        r_loc_all = [r for rr in r_loc for r in rr]
        r_ktl = [Res(f"ktl{i}") for i in range(9)]
        r_val = [Res(f"val{i}") for i in range(66)]
        rAF = Res("arena_fence")
        ARENA_ALL = ARENA_ALL + r_ktl + r_val

        def arena_fence():
            P.op("vector", lambda e: e.memset(fence_t[:, 0:4], 0.0), (), ARENA_ALL + [rAF])
        r_o = [[Res(f"o{h}_{c}") for c in range(17)] for h in range(12)]
        r_x1 = [Res(f"x1_{i}") for i in range(NT)]
        s_cc = P.new_sem("s_cc")
        s_k = P.new_sem("s_kload")
        s_v = P.new_sem("s_vload")
        out_entries = []

        def x_src(l, i):
            if i < NT_OWN:
                base = x_in if l == 0 else x1_d
                return base[i * 128:(i + 1) * 128, :]
            base = ctx_in if l == 0 else ctx1_d
            return base[(i - NT_OWN) * 128:(i - NT_OWN + 1) * 128, :]

        def x_dst(l, i):
            if i < NT_OWN:
                base = x1_d if l < n_layers - 1 else out_d
                return base[i * 128:(i + 1) * 128, :]
            return ctx1_d[(i - NT_OWN) * 128:(i - NT_OWN + 1) * 128, :]

        def main_body():
          for l in range(n_layers):
              last = (l == n_layers - 1)
              so_ = l * NS
              if l > 0:
                  set_phase("A")
              mod_pieces(l, 8, 12, False, True)
              res32 = cur["res32"]
              dma("sync", res32[0:2, :], bmod_in[l, 2048:3072].partition_broadcast(2), P.new_sem(), [], [rres])
              for hh in range(2):
                  tt(res32[0:2, hh * 512:(hh + 1) * 512], S[0][0:2, hh * 512:(hh + 1) * 512],
                     res32[0:2, hh * 512:(hh + 1) * 512], ALU.add, [rS[0], rres], [rres])
              for s_ in range(2):
                  for hh in range(2):
                      mm(U[1][:, 0:512], sel[0:2, s_, :], res32[0:2, hh * 512:(hh + 1) * 512], True, True,
                         [rres, rconst], [rU[1]])
                      cp(gate_bc[:, s_, hh * 512:(hh + 1) * 512], U[1][:, 0:512], [rU[1]], [rgate])

              if stop == 0:
                  raise _Stop
              arena_fence()
              load_cast(wKV, win_in[l, :, C_KV:C_KV + 768], 768, [rWKV], extra_r=[rAF])
              for i in range(NT):
                  s_ = 0 if i < NT_OWN else 1
                  xt, rxt, sxt = cur["X"].next()
                  dma("sync", xt, x_src(l, i), sxt, [r_x1[i]] if l > 0 else [], [rxt])
                  if s_ == 0:
                      rp, rrp, srp = cur["RP"].next()
                      dma("sync", rp, rope_in[i], srp, [], [rrp])
                  else:
                      rp, rrp = None, None
                  sq = cur["sq"]
                  act(sq[:, 0:D], xt, AF.Square, [rxt], [rsq])
                  red(ssx[:, 0:1], sq[:, 0:D], [rsq], [rssx])
                  rstd_inplace(ssx[:, 0:1], D, rssx)
                  xn, rxn, _ = XN.next()
                  act(xn, xt, AF.Identity, [rxt, rssx], [rxn], scale=ssx[:, 0:1])
                  if stop == 0.1:
                      raise _Stop
                  for fc in range(8):
                      tr(Ubf[0][:, fc * 128:(fc + 1) * 128], xn[:, fc * 128:(fc + 1) * 128], [rxn], [rU[0]])
                  hT, rhT, shT = HT.next()
                  tt(hT, Ubf[0][:, 0:1024].rearrange("p (f t) -> p f t", f=8),
                     gmod[:, l, s_, :].unsqueeze(2).to_broadcast([128, 8, 128]), ALU.mult, [rU[0], rmod], [rhT])
                  tt(hT, hT, shf[:, l, s_, :].unsqueeze(2).to_broadcast([128, 8, 128]), ALU.add, [rhT, rmod], [rhT])
                  dma("gpsimd", hn_d[i], hT, shT, [rhT], [r_hn[i]])
                  if stop == 0.2:
                      raise _Stop
                  for (c0, c1) in ((0, 512), (512, 768)):
                      for fc in range(8):
                          mm(S[0][:, c0:c1], hT[:, fc, :], wKV[:, fc, c0:c1], fc == 0, fc == 7, [rhT, rWKV], [rS[0]])
                  groups = [(0, 2, 64, so_ + S_GKA, 0), (128, 8, 32, so_ + S_GKC, 64)]
                  norm_rope(S[0][:, 0:384], 384, groups, rS[0], rp, rrp, rope=(s_ == 0))
                  if stop == 0.3:
                      raise _Stop
                  kr = cur["kr"]
                  vb, rvb, svb = VB.next()
                  act(vb[:, 0:128], S[0][:, 384:512], AF.Copy, [rS[0]], [rvb])
                  act(vb[:, 128:384], S[0][:, 512:768], AF.Copy, [rS[0]], [rvb])
                  for j in range(3):
                      tr(Ubf[1][:, j * 128:(j + 1) * 128], kr[:, j * 128:(j + 1) * 128], [rkr], [rU[1]])
                  kts, rkts, skts = KTS.next()
                  cp(kts, Ubf[1][:, 0:384].rearrange("p (j t) -> p j t", j=3), [rU[1]], [rkts])
                  if s_ == 0:
                      cols = slice(i * 128, (i + 1) * 128)
                      for j3 in range(3):
                          for half in range(2):
                              dma("gpsimd", lock_t[j3 * 2 + half].ap()[:, cols], kts[half * 64:(half + 1) * 64, j3, :],
                                  skts, [rkts], [r_loc[i][j3 * 2 + half]])
                      dma("gpsimd", vview(locv_t[i // 4].ap())[(i % 4) * 128:(i % 4 + 1) * 128, :], vb, svb, [rvb],
                          [r_loc[i][6]])
                  else:
                      ci = i - NT_OWN
                      cols = slice(ci * 128, (ci + 1) * 128)
                      dma("gpsimd", kx_d[0:128, cols], kts[:, 0, :], skts, [rkts], [r_kx[ci][0]])
                      dma("gpsimd", kx_d[128:384, cols].rearrange("(c p) t -> p c t", p=128), kts[:, 1:3, :], skts,
                          [rkts], [r_kx[ci][1]])
                      dma("gpsimd", vx_d[ci * 128:(ci + 1) * 128, :], vb, svb, [rvb], [r_kx[ci][2]])
                  if stop is not None and 0.4 <= stop < 0.5 and i == round((stop - 0.4) * 1000):
                      raise _Stop
              if stop == 1:
                  raise _Stop
              def allgather(src_t, dst_t):
                  P.dma("gpsimd", lambda e: e.collective_compute("AllGather", ALU.bypass,
                                                                 replica_groups=[[0, 1, 2, 3], [4, 5, 6, 7]],
                                                                 ins=[src_t.ap().opt()], outs=[dst_t.ap().opt()]),
                        s_cc, reads=r_loc_all, writes=[r_g], inc=1)

              for c_ in range(6):
                  allgather(lock_t[c_], gk_t[c_])
              for c_ in range(8):
                  allgather(locv_t[c_], gv_t[c_])

              if stop == 2:
                  raise _Stop
              chunks = [(c * 4, 4, False, c * 512, [(2 * p_, 2 * p_ + 1) for p_ in range(65)]) for c in range(8)]
              if not last:
                  chunks.append((NT_OWN, 2, True, OWN, [(128, 129)]))

              set_phase("T")
              for pname in ("A", "C0", "C1"):
                  if stop == 3 and pname == "C0":
                      raise _Stop
                  if pname == "A":
                      W, qc0, krow, vcol = 512, C_QA, 0, 0
                      groups = [(0, 8, 64, so_ + S_GQA, 0)]
                      streams = [(hk * 64, 64, g, hk, hk * 4 + g) for g in range(4) for hk in range(2)]
                      scale = 64 ** -0.5
                  else:
                      ci = int(pname[1])
                      W, qc0, krow, vcol = 128, C_QC + ci * 128, 128 + ci * 128, 128 + ci * 128
                      groups = [(0, 4, 32, so_ + S_GQC, 64)]
                      streams = [((hl * 2 + m) * 32, 32, 0, hl, 8 + ci * 2 + hl) for hl in range(2) for m in range(2)]
                      scale = 32 ** -0.5
                  nj = W // 128
                  load_cast(wS[:, :, 0:W], win_in[l, :, qc0:qc0 + W], W, [rWS])
                  arena_fence()
                  if pname == "A":
                      for k0 in range(0, NKT, 26):
                          mset(VA[:, k0:k0 + 26, :, 64:128], 1.0, r_val)
                  kc0 = krow // 64
                  for r_ in range(4):
                      for half in range(2):
                          dma("gpsimd", KT[half * 64:(half + 1) * 64, r_ * OWN:(r_ + 1) * OWN],
                              gk_t[kc0 + half].ap()[r_ * 64:(r_ + 1) * 64, :], s_k, [r_g, rAF], [r_ktl[r_ * 2 + half]])
                  dma("gpsimd", KT[:, SEQ:SEQ + CTX], kx_d[krow:krow + 128, :], s_k, r_kx_all + [rAF], [r_ktl[8]])
                  for r_ in range(4):
                      for c_ in range(8):
                          vsrc = vview(gv_t[c_].ap()[r_ * 48:(r_ + 1) * 48, :])
                          for h_ in range(2):
                              dma("gpsimd", VA[:, r_ * 32 + c_ * 4:r_ * 32 + c_ * 4 + 4, h_, 0:64],
                                  vsrc[:, vcol + h_ * 64:vcol + (h_ + 1) * 64].rearrange("(i p) d -> p i d", p=128),
                                  s_v, [r_g, rAF], [r_val[(r_ * 8 + c_) * 2 + h_]])
                  for h_ in range(2):
                      dma("gpsimd", VA[:, 128:130, h_, 0:64],
                          vx_d[:, vcol + h_ * 64:vcol + (h_ + 1) * 64].rearrange("(i p) d -> p i d", p=128), s_v,
                          r_kx_all + [rAF], [r_val[64 + h_]])

                  acc_i = 0
                  s_i = 0
                  for (t0, ntl, is_ctx, ocol, pairs) in chunks:
                      ncol = ntl * 128
                      hn, rhn, shn = cur["HN"].next()
                      dma("sync", hn[:, 0:ntl], hn_d[t0:t0 + ntl].rearrange("i p f t -> p i f t"), shn,
                          r_hn[t0:t0 + ntl], [rhn])
                      qT, rqT, _ = QT.next()
                      for ti in range(ntl):
                          if not is_ctx:
                              rp, rrp, srp = cur["RP"].next()
                              dma("sync", rp, rope_in[t0 + ti], srp, [], [rrp])
                          else:
                              rp, rrp = None, None
                          for fc in range(8):
                              mm(U[0][:, 0:W], hn[:, ti, fc, :], wS[:, fc, 0:W], fc == 0, fc == 7, [rhn, rWS], [rU[0]])
                          norm_rope(U[0][:, 0:W], W, groups, rU[0], rp, rrp, rope=not is_ctx)
                          kr = cur["kr"]
                          for j in range(nj):
                              tr(Ubf[1][:, j * 128:(j + 1) * 128], kr[:, j * 128:(j + 1) * 128], [rkr], [rU[1]])
                          cp(qT[:, 0:nj, ti * 128:(ti + 1) * 128], Ubf[1][:, 0:nj * 128].rearrange("p (j t) -> p j t", j=nj),
                             [rU[1]], [rqT])
                      for si, (prow, kd, qj, vh, ohead) in enumerate(streams):
                          acc, racc = ACC[acc_i % 2], rACC[acc_i % 2]
                          acc_i += 1
                          tp = (prow, 0) if prow == 96 else None
                          for pi, (ka, kb) in enumerate(pairs):
                              Sx, rSx = S[s_i % 2], rS[s_i % 2]
                              s_i += 1
                              for hh, kt in enumerate((ka, kb)):
                                  mm(Sx[:, hh * 512:hh * 512 + ncol], KT[prow:prow + kd, kt * 128:(kt + 1) * 128],
                                     qT[prow:prow + kd, qj, 0:ncol], True, True, r_ktl + [rqT], [rSx], tp=tp)
                              pt, rpt, _ = PT.next()
                              for hh in range(2):
                                  act(pt[:, hh * 512:hh * 512 + ncol], Sx[:, hh * 512:hh * 512 + ncol], AF.Exp, [rSx],
                                      [rpt], scale=scale)
                              for hh, kt in enumerate((ka, kb)):
                                  mm(acc[:, 0:ncol], VA[:, kt, vh, :], pt[:, hh * 512:hh * 512 + ncol],
                                     pi == 0 and hh == 0, pi == len(pairs) - 1 and hh == 1, r_val + [rpt], [racc])
                          rcp(R[64:128, 0:ncol], acc[64:128, 0:ncol], [racc], [rR])
                          if pname == "A":
                              onb, ronb, sonb = ONB.next()
                              tt(onb[0:64, 0:ncol], acc[0:64, 0:ncol], R[64:128, 0:ncol], ALU.mult, [racc, rR], [ronb])
                              dma("gpsimd", o_d[:, ohead, ocol:ocol + ncol], onb[0:64, 0:ncol], sonb, [ronb],
                                  [r_o[ohead][ocol // 256], r_o[ohead][ocol // 256 + (1 if ncol == 512 else 0)]])
                          else:
                              m_ = si % 2
                              tt(on32[0:64, m_, 0:ncol], acc[0:64, 0:ncol], R[64:128, 0:ncol], ALU.mult, [racc, rR],
                                 [ron[m_]])
                              if m_ == 1:
                                  stt(dif[0:64, 0:ncol], on32[0:64, 1, 0:ncol], lamw[0:64, l, 3:4], on32[0:64, 0, 0:ncol],
                                      ALU.mult, ALU.add, [ron[0], ron[1], rmod], [rdif])
                                  tt(dsq[0:64, 0:ncol], dif[0:64, 0:ncol], dif[0:64, 0:ncol], ALU.mult, [rdif], [rdsq])
                                  mm(U[0][0:64, 0:ncol], ones64[:, :], dsq[0:64, 0:ncol], True, True, [rdsq, rconst],
                                     [rU[0]])
                                  act(R[0:64, 0:ncol], U[0][0:64, 0:ncol], AF.Ln, [rU[0]], [rR], scale=1.0 / 64, bias=EPS)
                                  act(R[0:64, 0:ncol], R[0:64, 0:ncol], AF.Exp, [rR], [rR], scale=-0.5)
                                  tt(dif[0:64, 0:ncol], dif[0:64, 0:ncol], R[0:64, 0:ncol], ALU.mult, [rdif, rR], [rdif])
                                  onb, ronb, sonb = ONB.next()
                                  ts(onb[0:64, 0:ncol], dif[0:64, 0:ncol], lamw[0:64, l, 4:5], ALU.mult, [rdif, rmod],
                                     [ronb])
                                  dma("gpsimd", o_d[:, ohead, ocol:ocol + ncol], onb[0:64, 0:ncol], sonb, [ronb],
                                      [r_o[ohead][ocol // 256], r_o[ohead][ocol // 256 + (1 if ncol == 512 else 0)]])

              if stop == 4:
                  raise _Stop
              set_phase("M")
              arena_fence()
              load_cast(wM, win_in[l, :, C_M:C_M + 4352], 4352, [rWM], extra_r=[rAF])
              load_cast(wBV, win_in[l, :, C_BV:C_BV + 256], 256, [rWM], extra_r=[rAF])
              load_cast(wOUT, wout_in[l], 1024, [rWM], extra_r=[rAF])
              load_cast(wPA, wpa_in[l], 1024, [rWM], extra_r=[rAF], kdim=4)
              load_cast(wPC, wpc_in[l], 1024, [rWM], extra_r=[rAF], kdim=2)
              load_cast(wPB, wpb_in[l], 1024, [rWM], extra_r=[rAF], nparts=64, kdim=4)
              stg, rstg, sstg = cur["STG"].next()
              sv = stg[:, 0:512].rearrange("p (g q) -> p g q", g=4)
              dma("sync", sv, wsp_in[l], sstg, [], [rstg])
              cp(wsp, sv, [rstg], [rwsp])
              bsp_bc = small[0:64, so_ + S_BSP:so_ + S_BSP + 512].rearrange("p (g t) -> p g t", g=4)
              gsgu_bc = small[:, so_ + S_SGU:so_ + S_SGU + 256]
              sq, kn, res32 = cur["sq"], cur["kn"], cur["res32"]

              def sigmoid_from(ps_ap, r_ps, eb, rebx):
                  act(eb, ps_ap, AF.Exp, [r_ps], [rebx], scale=-1.0)
                  ts(eb, eb, 1.0, ALU.add, [rebx], [rebx])
                  rcp(eb, eb, [rebx], [rebx])

              mchunks = [(h * 2, 2, False, h * 256) for h in range(16)]
              if not last:
                  mchunks.append((NT_OWN, 2, True, OWN))
              for (t0, ntl, is_ctx, ocol) in mchunks:
                  ncol = NM
                  s_ = 1 if is_ctx else 0
                  hn, rhn, shn = cur["HN"].next()
                  dma("sync", hn, hn_d[t0:t0 + ntl].rearrange("i p f t -> p i f t"), shn, r_hn[t0:t0 + ntl], [rhn])
                  og, rog, sog = OG.next()
                  for par in range(2):
                      dma("sync", og[par * 64:(par + 1) * 64, :, :],
                          o_d[:, :, ocol:ocol + ncol].rearrange("d (c two) t -> d two c t", two=2)[:, par],
                          sog, [r_o[h][ocol // 256] for h in range(12)], [rog])

                  def proj(ps_ap, col0, ncols_w, r_ps):
                      for fc in range(8):
                          mm(ps_ap, wM[:, fc, col0:col0 + ncols_w], hn[:, :, fc, :], fc == 0, fc == 7, [rWM, rhn], [r_ps])

                  for ti in range(ntl):
                      for fc in range(8):
                          mm(U[0][:, 0:256], hn[:, ti, fc, :], wBV[:, fc, :], fc == 0, fc == 7, [rhn, rWM], [rU[0]])
                      act(sq[:, 0:256], U[0][:, 0:256], AF.Square, [rU[0]], [rsq])
                      red(ss[:, 0:4], sq[:, 0:256].rearrange("p (h d) -> p h d", h=4), [rsq], [rss])
                      rstd_inplace(ss[:, 0:4], 64, rss)
                      tt(kn[:, 0:256].rearrange("p (h d) -> p h d", h=4), U[0][:, 0:256].rearrange("p (h d) -> p h d", h=4),
                         ss[:, 0:4].unsqueeze(2).to_broadcast([128, 4, 64]), ALU.mult, [rU[0], rss], [rkn])
                      tt(vn, kn[:, 0:256], gsgu_bc, ALU.mult, [rkn, rconst], [rvn])
                      for g in range(4):
                          mm(U[1][0:64, g * 128:(g + 1) * 128], vn[:, g * 64:(g + 1) * 64], wsp[:, g, :], True, True,
                             [rvn, rwsp], [rU[1]])
                      tt(mixsb[0:64, :, ti * 128:(ti + 1) * 128], U[1][0:64, 0:512].rearrange("p (g t) -> p g t", g=4),
                         bsp_bc, ALU.add, [rU[1], rconst], [rmix])

                  zi = 0
                  for blk in range(6):
                      zps, rz = ACC[zi % 2], rACC[zi % 2]
                      col0 = blk * 128 if blk < 4 else 768 + (blk - 4) * 128
                      proj(zps[:, 0:ncol], col0, 128, rz)
                      eb, tb = ebuf[:, zi % 2, :], tbuf[:, zi % 2, :]
                      sigmoid_from(zps[:, 0:ncol], rz, eb, reb[zi % 2])
                      tt(tb, zps[:, 0:ncol], eb, ALU.mult, [rz, reb[zi % 2]], [rtb[zi % 2]])
                      tt(G_AC[:, blk, :], tb, og[:, blk, :], ALU.mult, [rtb[zi % 2], rog], [rG])
                      zi += 1
                  for g in range(4):
                      zps, rz = ACC[zi % 2], rACC[zi % 2]
                      proj(zps[0:64, 0:ncol], 512 + g * 64, 64, rz)
                      eb, tb = ebuf[0:64, zi % 2, :], tbuf[0:64, zi % 2, :]
                      sigmoid_from(zps[0:64, 0:ncol], rz, eb, reb[zi % 2])
                      tt(tb, zps[0:64, 0:ncol], eb, ALU.mult, [rz, reb[zi % 2]], [rtb[zi % 2]])
                      ups, ru = U[g % 2], rU[g % 2]
                      proj(ups[0:64, 0:ncol], 1024 + g * 64, 64, ru)
                      tt(eb, ups[0:64, 0:ncol], mixsb[0:64, g, :], ALU.mult, [ru, rmix], [reb[zi % 2]])
                      tt(G_B[0:64, g, :], tb, eb, ALU.mult, [rtb[zi % 2], reb[zi % 2]], [rG])
                      zi += 1

                  for n in range(8):
                      nb = slice(n * 128, (n + 1) * 128)
                      ysl = [(S[0][:, 0:ncol], rS[0]), (S[0][:, 512:512 + ncol], rS[0]), (S[1][:, 0:ncol], rS[1])]
                      for c_ in range(4):
                          mm(ysl[0][0], wPA[:, c_, nb], G_AC[:, c_, :], c_ == 0, c_ == 3, [rWM, rG], [ysl[0][1]])
                      for g in range(4):
                          mm(ysl[1][0], wPB[0:64, g, nb], G_B[0:64, g, :], g == 0, g == 3, [rWM, rG], [ysl[1][1]])
                      for c_ in range(2):
                          mm(ysl[2][0], wPC[:, c_, nb], G_AC[:, 4 + c_, :], c_ == 0, c_ == 1, [rWM, rG], [ysl[2][1]])
                      for br in range(3):
                          gi = (n * 3 + br) % 2
                          gps, rg_ = ACC[gi], rACC[gi]
                          proj(gps[:, 0:ncol], 1280 + br * 1024 + n * 128, 128, rg_)
                          sigmoid_from(gps[:, 0:ncol], rg_, sig[:, br, :], rsig[br])
                      tt(macc, ysl[0][0], sig[:, 0, :], ALU.mult, [ysl[0][1], rsig[0]], [rmacc])
                      tt(tbuf[:, 0, :], ysl[1][0], sig[:, 1, :], ALU.mult, [ysl[1][1], rsig[1]], [rtb[0]])
                      tt(macc, macc, tbuf[:, 0, :], ALU.add, [rmacc, rtb[0]], [rmacc])
                      tt(tbuf[:, 1, :], ysl[2][0], sig[:, 2, :], ALU.mult, [ysl[2][1], rsig[2]], [rtb[1]])
                      tt(mT[:, n, :], macc, tbuf[:, 1, :], ALU.add, [rmacc, rtb[1]], [rmT])

                  for ti in range(ntl):
                      i = t0 + ti
                      xt, rxt, sxt = cur["X"].next()
                      dma("sync", xt, x_src(l, i), sxt, [r_x1[i]] if l > 0 else [], [rxt])
                      for hh in range(2):
                          for k_ in range(8):
                              mm(S[1][:, hh * 512:(hh + 1) * 512], mT[:, k_, ti * 128:(ti + 1) * 128],
                                 wOUT[:, k_, hh * 512:(hh + 1) * 512], k_ == 0, k_ == 7, [rmT, rWM], [rS[1]])
                      for hh in range(2):
                          tt(res32[:, hh * 512:(hh + 1) * 512], S[1][:, hh * 512:(hh + 1) * 512],
                             gate_bc[:, s_, hh * 512:(hh + 1) * 512], ALU.mult, [rS[1], rgate], [rres])
                      tt(xt, xt, res32, ALU.add, [rxt, rres], [rxt])
                      e = dma("gpsimd", x_dst(l, i), xt, sxt, [rxt], [r_x1[i]])
                      if last:
                          out_entries.append(e)


        try:
            main_body()
        except _Stop:
            pass
        if not out_entries:
            out_entries.append(dma("sync", out_d[0:128, 0:16], ss[:, 0:16], P.new_sem(), [rss], []))
        for e in out_entries:
            P.wait_at_end(e)
        stats = P.emit()
    return nc, stats


def _rope_tables(j):
    n = j * OWN + np.arange(OWN)
    rows = (n // 64).astype(np.float32)
    cols = (n % 64).astype(np.float32)
    out = np.zeros((OWN, 96), np.float32)

    def tab(dim):
        quarter = dim // 4
        inv = np.power(np.float32(10000.0), -np.arange(quarter, dtype=np.float32) / quarter).astype(np.float32)
        ang = np.concatenate([rows[:, None] * inv, cols[:, None] * inv], axis=-1).astype(np.float32)
        return np.cos(ang).astype(np.float32), np.sin(ang).astype(np.float32)

    ca, sa = tab(64)
    cq, sq_ = tab(32)
    out[:, 0:32], out[:, 32:64], out[:, 64:80], out[:, 80:96] = ca, sa, cq, sq_
    return np.ascontiguousarray(out.reshape(NT_OWN, 128, 96))


def _perm_cols():
    r = np.arange
    qa = np.array([(hk * 4 + g) * 64 + d for g in range(4) for hk in range(2) for d in range(64)])
    return np.concatenate([r(768, 896), r(1024, 1280), r(896, 1024), r(1280, 1536), qa, r(512, 768),
                           r(1536, 2048), r(2816, 3072), r(2048, 2304), r(2304, 2560), r(3072, 6144), r(2560, 2816)])


_CACHE = {}


def kernel(x, c, ctx, c_ctx, w_mod, b_mod, g_norm, w_in, gq_a, gk_a, gq_c, gk_c,
           g_sgu, w_sp, b_sp, lam_c, g_subln, w_pa, w_pb, w_pc, w_out):
    f = lambda a: np.ascontiguousarray(np.asarray(a, dtype=np.float32))
    x, c, ctx, c_ctx = f(x), f(c), f(ctx), f(c_ctx)
    w_mod, b_mod, g_norm, w_in = f(w_mod), f(b_mod), f(g_norm), f(w_in)
    L = w_in.shape[0]
    if "nc" not in _CACHE:
        _CACHE["nc"] = build_program(L)
    nc, _ = _CACHE["nc"]
    w_in_p = np.ascontiguousarray(w_in[:, :, _perm_cols()])
    small = np.concatenate([f(gq_a), f(gk_a), f(gq_c), f(gk_c), f(g_sgu).reshape(L, 256),
                            f(b_sp).reshape(L, 512)], axis=1)
    shared = {
        "w_mod": w_mod,
        "bmT": np.ascontiguousarray(b_mod.reshape(L, 24, 128).transpose(0, 2, 1)),
        "b_mod": b_mod,
        "gn": np.ascontiguousarray(g_norm.reshape(L, 8, 128).transpose(0, 2, 1)),
        "w_in": w_in_p,
        "small": np.ascontiguousarray(small),
        "gsub": np.ascontiguousarray(np.tile(f(g_subln), (1, 2)).T),
        "lam": np.ascontiguousarray(f(lam_c).reshape(L, 128)),
        "w_spT": np.ascontiguousarray(f(w_sp).transpose(0, 3, 1, 2)),
        "w_pa": f(w_pa),
        "w_pb": f(w_pb),
        "w_pc": f(w_pc),
        "w_out": f(w_out),
        "ident": np.eye(128, dtype=np.float32).astype(ml_dtypes.bfloat16),
        "sel": np.ascontiguousarray(np.stack([np.stack([np.ones(128), np.zeros(128)]),
                                              np.stack([np.zeros(128), np.ones(128)])], axis=1).astype(np.float32)),
    }
    in_maps = []
    for core in range(NCORES):
        b, j = core // 4, core % 4
        cc2 = np.stack([c[b], c_ctx], axis=-1).reshape(8, 128, 2).transpose(1, 0, 2).reshape(128, 16)
        m = dict(shared)
        m["x"] = np.ascontiguousarray(x[b, j * OWN:(j + 1) * OWN])
        m["ctx"] = np.ascontiguousarray(ctx[b])
        m["cc"] = np.ascontiguousarray(cc2)
        m["rope"] = _rope_tables(j)
        in_maps.append(m)
    res = run_bass_kernel_spmd(nc, in_maps, core_ids=list(range(NCORES)))
    out = np.empty((2, SEQ, D), np.float32)
    for core in range(NCORES):
        b, j = core // 4, core % 4
        out[b, j * OWN:(j + 1) * OWN] = res.results[core]["out"]
    return out
```

```python
import contextlib
import math
import numpy as np
import ml_dtypes
import concourse.bass as bass
import concourse.mybir as mybir
from concourse.bass_utils import run_bass_kernel_spmd

F32 = mybir.dt.float32
BF16 = mybir.dt.bfloat16
ALU = mybir.AluOpType
AF = mybir.ActivationFunctionType
AX = mybir.AxisListType

NCORES = 8
D = 1024
SEQ = 16384
OWN = 4096
CTX = 256
NT_OWN = 32
NT = 34
EPS = 1e-6
NKT = 130

C_KV = 0
C_QA = 768
C_QC = 1280
C_M = 1536
C_BV = 5888
S_GQA, S_GKA, S_GQC, S_GKC, S_SGU, S_BSP, NS = 0, 64, 128, 160, 192, 448, 960

COMPUTE = ("tensor", "vector", "scalar", "gpsimd")
QUEUES = ("sync", "tensor", "vector", "scalar", "gpsimd")


class Res:
    __slots__ = ("name", "last_w", "readers", "excl")

    def __init__(self, name="", excl=False):
        self.name = name
        self.last_w = None
        self.readers = {}
        self.excl = excl


class Entry:
    __slots__ = ("eng", "fn", "waits", "signal", "val", "sem", "inc", "kind")

    def __init__(self, eng, fn, kind):
        self.eng = eng
        self.fn = fn
        self.waits = []
        self.signal = False
        self.val = None
        self.sem = None
        self.inc = 0
        self.kind = kind


class Prog:
    def __init__(self, nc, stack, same_engine_sync=True):
        self.nc = nc
        self.stack = stack
        self.q = {e: [] for e in QUEUES}
        self.esem = {e: stack.enter_context(nc.semaphore("es_" + e)) for e in COMPUTE}
        self.same = same_engine_sync
        self.nsem = 0
        self.final = []

    def new_sem(self, name=None):
        self.nsem += 1
        return [self.stack.enter_context(self.nc.semaphore(name or f"ds{self.nsem}")), 0]

    def _deps(self, e, reads, writes):
        ex = [r for r in reads if r.excl]
        if ex:
            reads = [r for r in reads if not r.excl]
            writes = list(writes) + ex
        deps = []
        for r in reads:
            if r.last_w is not None:
                deps.append(r.last_w)
        for w in writes:
            if w.last_w is not None:
                deps.append(w.last_w)
            deps.extend(w.readers.values())
        seen = set()
        for d in deps:
            if d is e or id(d) in seen:
                continue
            seen.add(id(d))
            if d.kind == "c":
                if d.eng == e.eng and e.kind == "c" and (d.eng == "tensor" or not self.same):
                    continue
                d.signal = True
            e.waits.append(d)
        for r in reads:
            key = e.eng if e.kind == "c" else id(e)
            r.readers[key] = e
        for w in writes:
            w.last_w = e
            w.readers = {}

    def op(self, eng, fn, reads=(), writes=()):
        e = Entry(eng, fn, "c")
        self._deps(e, reads, writes)
        self.q[eng].append(e)
        return e

    def dma(self, eng, fn, sem, reads=(), writes=(), inc=16):
        e = Entry(eng, fn, "d")
        self._deps(e, reads, writes)
        sem[1] += inc
        e.sem = sem[0]
        e.inc = inc
        e.val = sem[1]
        self.q[eng].append(e)
        return e

    def wait_at_end(self, e):
        if e.kind == "c":
            e.signal = True
        self.final.append(e)

    def emit(self):
        for eng in COMPUTE:
            c = 0
            for e in self.q[eng]:
                if e.kind == "c" and e.signal:
                    c += 1
                    e.val = c
        stats = {}
        with self.nc.Block() as block:
            for eng in QUEUES:
                entries = self.q[eng]
                if not entries and eng != "sync":
                    continue
                stats[eng] = len(entries)

                def body(engine, entries=entries, eng=eng):
                    waited = {}

                    def wait(d):
                        sem = self.esem[d.eng] if d.kind == "c" else d.sem
                        k = id(sem)
                        if waited.get(k, 0) >= d.val:
                            return
                        waited[k] = d.val
                        engine.wait_ge(sem, d.val)

                    for e in entries:
                        for d in e.waits:
                            wait(d)
                        ins = e.fn(engine)
                        if e.kind == "c":
                            if e.signal:
                                ins.then_inc(self.esem[eng], 1)
                        else:
                            ins.then_inc(e.sem, e.inc)
                    if eng == "sync":
                        for d in self.final:
                            wait(d)

                getattr(block, eng)(body)
        return stats


class _Stop(Exception):
    pass


def build_program(n_layers=2, stop=None):
    nc = bass.Bass("TRN2", target_bir_lowering=False)
    dt = lambda name, shape, dtp, kind=None: (nc.dram_tensor(name, shape, dtp, kind=kind) if kind
                                              else nc.dram_tensor(name, shape, dtp))
    x_in = dt("x", [OWN, D], F32, "ExternalInput").ap()
    ctx_in = dt("ctx", [CTX, D], F32, "ExternalInput").ap()
    cc_in = dt("cc", [128, 16], F32, "ExternalInput").ap()
    wmod_in = dt("w_mod", [2, D, 3 * D], F32, "ExternalInput").ap()
    bmT_in = dt("bmT", [2, 128, 24], F32, "ExternalInput").ap()
    bmod_in = dt("b_mod", [2, 3 * D], F32, "ExternalInput").ap()
    gn_in = dt("gn", [2, 128, 8], F32, "ExternalInput").ap()
    win_in = dt("w_in", [2, D, 6144], F32, "ExternalInput").ap()
    small_in = dt("small", [2, NS], F32, "ExternalInput").ap()
    gsub_in = dt("gsub", [128, 2], F32, "ExternalInput").ap()
    lam_in = dt("lam", [2, 128], F32, "ExternalInput").ap()
    wsp_in = dt("w_spT", [2, 128, 4, 128], F32, "ExternalInput").ap()
    wpa_in = dt("w_pa", [2, 512, D], F32, "ExternalInput").ap()
    wpb_in = dt("w_pb", [2, 256, D], F32, "ExternalInput").ap()
    wpc_in = dt("w_pc", [2, 256, D], F32, "ExternalInput").ap()
    wout_in = dt("w_out", [2, D, D], F32, "ExternalInput").ap()
    rope_in = dt("rope", [NT_OWN, 128, 96], F32, "ExternalInput").ap()
    ident_in = dt("ident", [128, 128], BF16, "ExternalInput").ap()
    sel_in = dt("sel", [2, 2, 128], F32, "ExternalInput").ap()
    out_d = dt("out", [OWN, D], F32, "ExternalOutput").ap()

    x1_d = dt("x1", [OWN, D], F32).ap()
    ctx1_d = dt("ctx1", [CTX, D], F32).ap()
    hn_d = dt("hn_scr", [NT, 128, 8, 128], BF16).ap()
    o_d = dt("o_scr", [64, 12, OWN + CTX], BF16).ap()
    lock_t = [dt(f"lock{c}", [64, OWN], BF16) for c in range(6)]
    gk_t = [dt(f"gk{c}", [256, OWN], BF16) for c in range(6)]
    locv_t = [dt(f"locv{c}", [48, OWN], BF16) for c in range(8)]
    gv_t = [dt(f"gv{c}", [192, OWN], BF16) for c in range(8)]
    kx_d = dt("kx", [384, CTX], BF16).ap()
    vx_d = dt("vx", [CTX, 384], BF16).ap()

    def vview(ap2d):
        return ap2d.rearrange("r c -> (r c)").rearrange("(t f) -> t f", f=384)

    with contextlib.ExitStack() as st:
        P = Prog(nc, st)
        sb = lambda n, s, d: st.enter_context(nc.sbuf_tensor("sb_" + n, s, d))
        pst = lambda n, s, d: st.enter_context(nc.psum_tensor(n, s, d))

        ARENA_N = 55296
        arena = sb("arena", [128, ARENA_N], BF16)
        KT = arena[:, 0:16640]
        VA = arena[:, 16640:16640 + NKT * 256].rearrange("p (k h c) -> p k h c", k=NKT, h=2)
        wKV = arena[:, 0:6144].rearrange("p (f n) -> p f n", f=8)
        wM = arena[:, 0:8 * 4352].rearrange("p (f n) -> p f n", f=8)
        wBV = arena[:, 34816:34816 + 2048].rearrange("p (f n) -> p f n", f=8)
        wOUT = arena[:, 36864:36864 + 8192].rearrange("p (f n) -> p f n", f=8)
        wPA = arena[:, 45056:45056 + 4096].rearrange("p (c n) -> p c n", c=4)
        wPC = arena[:, 49152:49152 + 2048].rearrange("p (c n) -> p c n", c=2)
        wPB = arena[0:64, 51200:51200 + 4096].rearrange("p (c n) -> p c n", c=4)
        rKT, rVA, rWM, rWKV = Res("KT"), Res("VA"), Res("wM"), Res("wKV")
        ARENA_ALL = [rKT, rVA, rWM, rWKV]
        ident = sb("ident", [128, 128], BF16)
        ones64 = sb("ones64", [64, 64], F32)
        sel = sb("sel", [2, 2, 128], F32)
        cc = sb("cc", [128, 16], F32)
        silc = sb("silc", [128, 16], F32)
        small = sb("small", [128, 2 * NS], F32)
        modT = sb("modT", [128, 2, 24, 2], F32)
        bmT = sb("bmT", [128, 2, 24], F32)
        gn = sb("gn", [128, 2, 8], F32)
        gmod = sb("gmod", [128, 2, 2, 8], F32)
        shf = sb("shf", [128, 2, 2, 8], F32)
        gate_bc = sb("gate_bc", [128, 2, 1024], F32)
        lamt = sb("lamt", [128, 2, 128], F32)
        lamw = sb("lamw", [128, 2, 8], F32)
        gsub = sb("gsub", [128, 2], F32)
        ss = sb("ss", [128, 16], F32)
        ssx = sb("ssx", [128, 2], F32)
        rconst, rmod, rgate, rss, rssx = Res("const"), Res("mod"), Res("gate_bc"), Res("ss"), Res("ssx")

        SCR_BYTES = 57344
        scr = sb("scr", [128, SCR_BYTES // 2], BF16)

        class Carver:
            def __init__(self):
                self.off = 0

            def take(self, nfree, dtype, parts=128):
                n2 = nfree * (2 if dtype == F32 else 1)
                n2 = (n2 + 15) // 16 * 16
                assert self.off + n2 <= SCR_BYTES // 2, (self.off, n2)
                ap = scr[0:parts, self.off:self.off + n2]
                self.off += n2
                if dtype == F32:
                    ap = ap.bitcast(F32)
                return ap[:, 0:nfree]

        class Rot:
            def __init__(self, name, aps):
                self.slots = [(a, Res(f"{name}{i}"), P.new_sem()) for i, a in enumerate(aps)]
                self.i = 0

            def next(self):
                s_ = self.slots[self.i % len(self.slots)]
                self.i += 1
                return s_

            def res(self):
                return [s_[1] for s_ in self.slots]

        cA = Carver()
        X_A = Rot("xA", [cA.take(D, F32) for _ in range(2)])
        XN = Rot("xn", [cA.take(D, BF16)])
        HT = Rot("hT", [cA.take(1024, BF16).rearrange("p (f t) -> p f t", f=8) for _ in range(2)])
        STG_A = Rot("stgA", [cA.take(1024, F32) for _ in range(2)])
        RP_A = Rot("rpA", [cA.take(96, F32) for _ in range(2)])
        KTS = Rot("kts", [cA.take(384, BF16).rearrange("p (j t) -> p j t", j=3) for _ in range(2)])
        VB = Rot("vb", [cA.take(384, BF16) for _ in range(2)])
        sq_A, kn_A, kr_A = cA.take(1024, F32), cA.take(512, F32), cA.take(512, BF16)
        rt1_A, rt2_A = cA.take(256, F32), cA.take(256, F32)
        res32_A = cA.take(1024, F32)
        cT = Carver()
        HN_T = Rot("hnT", [cT.take(4096, BF16).rearrange("p (i f t) -> p i f t", i=4, f=8)])
        STG_T = Rot("stgT", [cT.take(1024, F32) for _ in range(2)])
        RP_T = Rot("rpT", [cT.take(96, F32) for _ in range(2)])
        wS = cT.take(4096, BF16).rearrange("p (f n) -> p f n", f=8)
        rWS = Res("wS")
        sq_T, kn_T, kr_T = cT.take(512, F32), cT.take(512, F32), cT.take(512, BF16)
        rt1_T, rt2_T = cT.take(256, F32), cT.take(256, F32)
        PT = Rot("pt", [cT.take(1024, BF16) for _ in range(3)])
        QT = Rot("qT", [cT.take(2048, BF16).rearrange("p (j t) -> p j t", j=4)])
        ONB = Rot("onb", [cT.take(512, BF16) for _ in range(2)])
        R = cT.take(512, F32)
        on32 = cT.take(1024, F32).rearrange("p (m t) -> p m t", m=2)
        dif, dsq = cT.take(512, F32), cT.take(512, F32)
        rR, ron, rdif, rdsq = Res("R"), [Res("on0"), Res("on1")], Res("dif"), Res("dsq")
        cM = Carver()
        NM = 256
        HN_M = Rot("hnM", [cM.take(2048, BF16).rearrange("p (i f t) -> p i f t", i=2, f=8) for _ in range(2)])
        STG_M = Rot("stgM", [cM.take(1024, F32) for _ in range(2)])
        X_M = Rot("xM", [cM.take(D, F32) for _ in range(2)])
        OG = Rot("og", [cM.take(6 * NM, BF16).rearrange("p (c t) -> p c t", c=6)])
        sq_M, kn_M = cM.take(256, F32), cM.take(256, F32)
        G_AC = cM.take(6 * NM, BF16).rearrange("p (c t) -> p c t", c=6)
        G_B = cM.take(4 * NM, BF16).rearrange("p (c t) -> p c t", c=4)
        mT = cM.take(8 * NM, BF16).rearrange("p (c t) -> p c t", c=8)
        mixsb = cM.take(4 * NM, F32).rearrange("p (c t) -> p c t", c=4)
        sig = cM.take(3 * NM, F32).rearrange("p (c t) -> p c t", c=3)
        ebuf = cM.take(2 * NM, F32).rearrange("p (c t) -> p c t", c=2)
        tbuf = cM.take(2 * NM, F32).rearrange("p (c t) -> p c t", c=2)
        res32_M = cM.take(1024, F32)
        macc = cM.take(NM, F32)
        vn = cM.take(256, BF16)
        wsp = cM.take(512, BF16).rearrange("p (g q) -> p g q", g=4)
        rG, rmT, rmix, rvn, rwsp, rmacc, rres = (Res("G"), Res("mT"), Res("mix"), Res("vn"), Res("wsp"),
                                                 Res("macc"), Res("res32"))
        reb, rtb, rsig = [Res("eb0"), Res("eb1")], [Res("tb0"), Res("tb1")], [Res("sg0"), Res("sg1"), Res("sg2")]
        rsq, rkn, rkr, rrt = Res("sq"), Res("kn"), Res("kr"), Res("rt")
        SCR_RES = ([rsq, rkn, rkr, rrt, rres, rWS, rR, rdif, rdsq, rG, rmT, rmix, rvn, rwsp, rmacc] + ron + reb + rtb
                   + rsig)
        for rot in (X_A, XN, HT, STG_A, RP_A, KTS, VB, HN_T, STG_T, RP_T, PT, QT, ONB, HN_M, STG_M, X_M, OG):
            SCR_RES += rot.res()
        fence_t = sb("fence_t", [128, 8], F32)
        rfence = Res("fence")

        def fence():
            P.op("vector", lambda e: e.memset(fence_t[:], 0.0), (), SCR_RES + [rfence])

        cur = {}

        def set_phase(ph):
            fence()
            if ph == "A":
                cur.update(X=X_A, STG=STG_A, RP=RP_A, sq=sq_A, kn=kn_A, kr=kr_A, rt1=rt1_A, rt2=rt2_A, res32=res32_A)
            elif ph == "T":
                cur.update(HN=HN_T, STG=STG_T, RP=RP_T, sq=sq_T, kn=kn_T, kr=kr_T, rt1=rt1_T, rt2=rt2_T)
            else:
                cur.update(HN=HN_M, STG=STG_M, X=X_M, sq=sq_M, kn=kn_M, res32=res32_M)

        set_phase("A")

        S = [pst("S0", [128, 1024], F32), pst("S1", [128, 1024], F32)]
        rS = [Res("S0", True), Res("S1", True)]
        ACC = [pst("acc0", [128, 512], F32), pst("acc1", [128, 512], F32)]
        rACC = [Res("acc0", True), Res("acc1", True)]
        U = [pst("U0", [128, 512], F32), pst("U1", [128, 512], F32)]
        rU = [Res("U0", True), Res("U1", True)]
        Ubf = [u[:].bitcast(BF16) for u in U]

        def mm(out, lhsT, rhs, start, stop, r, w, tp=None):
            if tp is None:
                return P.op("tensor", lambda e: e.matmul(out, lhsT=lhsT, rhs=rhs, start=start, stop=stop), r, w)
            return P.op("tensor", lambda e: e.matmul(out, lhsT=lhsT, rhs=rhs, start=start, stop=stop,
                                                     tile_position=tp), r, w)

        def tr(out, in_, r, w):
            return P.op("tensor", lambda e: e.transpose(out, in_, ident[:]), list(r) + [rconst], w)

        def act(out, in_, func, r, w, scale=1.0, bias=0.0):
            return P.op("scalar", lambda e: e.activation(out=out, in_=in_, func=func, bias=bias, scale=scale), r, w)

        def tt(out, in0, in1, op, r, w, eng="vector"):
            return P.op(eng, lambda e: e.tensor_tensor(out=out, in0=in0, in1=in1, op=op), r, w)

        def ts(out, in0, s1, op0, r, w, s2=None, op1=None, eng="vector"):
            if op1 is None:
                return P.op(eng, lambda e: e.tensor_scalar(out=out, in0=in0, scalar1=s1, scalar2=None, op0=op0), r, w)
            return P.op(eng, lambda e: e.tensor_scalar(out=out, in0=in0, scalar1=s1, scalar2=s2, op0=op0, op1=op1), r, w)

        def stt(out, in0, scalar, in1, op0, op1, r, w):
            return P.op("vector", lambda e: e.scalar_tensor_tensor(out=out, in0=in0, scalar=scalar, in1=in1,
                                                                   op0=op0, op1=op1), r, w)

        def cp(out, in_, r, w, eng="vector"):
            return P.op(eng, lambda e: e.tensor_copy(out=out, in_=in_), r, w)

        def red(out, in_, r, w):
            return P.op("vector", lambda e: e.tensor_reduce(out=out, in_=in_, axis=AX.X, op=ALU.add), r, w)

        def rcp(out, in_, r, w):
            return P.op("vector", lambda e: e.reciprocal(out=out, in_=in_), r, w)

        def dma(q, out, in_, sem, r, w):
            return P.dma(q, lambda e: e.dma_start(out=out, in_=in_), sem, r, w)

        def mset(ap, val, w):
            return P.op("vector", lambda e: e.memset(ap, val), (), w)

        def rstd_inplace(ap, d, r):
            act(ap, ap, AF.Ln, [r], [r], scale=1.0 / d, bias=EPS)
            act(ap, ap, AF.Exp, [r], [r], scale=-0.5)


        dma("sync", ident[:], ident_in, P.new_sem(), [], [rconst])
        dma("sync", sel[:], sel_in, P.new_sem(), [], [rconst])
        dma("sync", cc[:], cc_in, P.new_sem(), [], [rconst])
        dma("sync", small[:], small_in.rearrange("l n -> (l n)").partition_broadcast(128), P.new_sem(), [], [rconst])
        dma("sync", bmT[:], bmT_in.rearrange("l p k -> p l k"), P.new_sem(), [], [rconst])
        dma("sync", gn[:], gn_in.rearrange("l p k -> p l k"), P.new_sem(), [], [rconst])
        dma("sync", lamt[:], lam_in.rearrange("l n -> (l n)").partition_broadcast(128), P.new_sem(), [], [rconst])
        dma("sync", gsub[:], gsub_in, P.new_sem(), [], [rconst])
        mset(ones64[:], 1.0, [rconst])

        act(silc[:], cc[:], AF.Exp, [rconst], [rmod], scale=-1.0)
        ts(silc[:], silc[:], 1.0, ALU.add, [rmod], [rmod])
        rcp(silc[:], silc[:], [rmod], [rmod])
        tt(silc[:], silc[:], cc[:], ALU.mult, [rmod, rconst], [rmod])

        def mod_pieces(l, n_lo, n_hi, feat, rows):
            for pc in range(n_lo * 2, n_hi * 2):
                stg, rstg, sstg = cur["STG"].next()
                sv = stg[:, 0:1024].rearrange("p (k n) -> p k n", k=8)
                c0 = pc * 128
                dma("sync", sv, wmod_in[l, :, c0:c0 + 128].rearrange("(k p) n -> p k n", p=128), sstg, [], [rstg])
                if feat:
                    for kc in range(8):
                        mm(U[0][:, pc * 2:pc * 2 + 2], sv[:, kc, :], silc[:, kc * 2:kc * 2 + 2], kc == 0, kc == 7,
                           [rstg, rmod], [rU[0]])
                if rows:
                    r0 = c0 - 2048
                    for kc in range(8):
                        mm(S[0][0:2, r0:r0 + 128], silc[:, kc * 2:kc * 2 + 2], sv[:, kc, :], kc == 0, kc == 7,
                           [rstg, rmod], [rS[0]])

        for l in range(n_layers):
            mod_pieces(l, 0, 12, True, False)
            tt(modT[:, l, :, :], U[0][:, 0:48].rearrange("p (k s) -> p k s", s=2),
               bmT[:, l, :].unsqueeze(2).to_broadcast([128, 24, 2]), ALU.add, [rU[0], rconst], [rmod])
            for s_ in range(2):
                ts(gmod[:, l, s_, :], modT[:, l, 8:16, s_], 1.0, ALU.add, [rmod], [rmod])
                tt(gmod[:, l, s_, :], gmod[:, l, s_, :], gn[:, l, :], ALU.mult, [rmod, rconst], [rmod])
                cp(shf[:, l, s_, :], modT[:, l, 0:8, s_], [rmod], [rmod])
            lam_init = 0.8 - 0.6 * math.exp(-0.3 * l)
            lv = lamt[:, l, :].rearrange("p (a b d) -> p a b d", a=2, b=2)
            sqv = cur["sq"][:, 0:64].rearrange("p (a d) -> p a d", a=2)
            tt(sqv, lv[:, :, 0, :], lv[:, :, 1, :], ALU.mult, [rconst], [rsq])
            red(lamw[:, l, 0:2], sqv, [rsq], [rmod])
            act(lamw[:, l, 0:2], lamw[:, l, 0:2], AF.Exp, [rmod], [rmod])
            tt(lamw[:, l, 2:3], lamw[:, l, 0:1], lamw[:, l, 1:2], ALU.subtract, [rmod], [rmod])
            ts(lamw[:, l, 3:4], lamw[:, l, 2:3], -1.0, ALU.mult, [rmod], [rmod], s2=-lam_init, op1=ALU.add)
            ts(lamw[:, l, 4:5], gsub[:, l:l + 1], 1.0 - lam_init, ALU.mult, [rconst], [rmod])

        def load_cast(dst3, src2, ncols, wres, nparts=128, kdim=8, extra_r=()):
            step = 1024 // kdim
            for c0 in range(0, ncols, step):
                cw = min(step, ncols - c0)
                stg, rstg, sstg = cur["STG"].next()
                sv = stg[0:nparts, 0:kdim * cw].rearrange("p (k n) -> p k n", k=kdim)
                dma("sync", sv, src2[:, c0:c0 + cw].rearrange("(k p) n -> p k n", p=nparts), sstg, [], [rstg])
                cp(dst3[:, :, c0:c0 + cw], sv, [rstg] + list(extra_r), wres)

        def norm_rope(ps_ap, width, groups, r_ps, rp, rrp, rope):
            sq, kn, kr, rt1, rt2 = cur["sq"], cur["kn"], cur["kr"], cur["rt1"], cur["rt2"]
            act(sq[:, 0:width], ps_ap, AF.Square, [r_ps], [rsq])
            so = 0
            for (c0, H, d, goff, ro) in groups:
                red(ss[:, so:so + H], sq[:, c0:c0 + H * d].rearrange("p (h d) -> p h d", h=H), [rsq], [rss])
                act(ss[:, so:so + H], ss[:, so:so + H], AF.Ln, [rss], [rss], scale=1.0 / d, bias=EPS)
                so += H
            act(ss[:, 0:so], ss[:, 0:so], AF.Exp, [rss], [rss], scale=-0.5)
            so = 0
            for (c0, H, d, goff, ro) in groups:
                v = kn[:, c0:c0 + H * d].rearrange("p (h d) -> p h d", h=H)
                tt(v, ps_ap[:, c0:c0 + H * d].rearrange("p (h d) -> p h d", h=H),
                   ss[:, so:so + H].unsqueeze(2).to_broadcast([128, H, d]), ALU.mult, [r_ps, rss], [rkn])
                tt(v, v, small[:, goff:goff + d].unsqueeze(1).to_broadcast([128, H, d]), ALU.mult,
                   [rkn, rconst], [rkn])
                so += H
            if not rope:
                cp(kr[:, 0:width], kn[:, 0:width], [rkn], [rkr])
                return
            for (c0, H, d, goff, ro) in groups:
                hd = d // 2
                v = kn[:, c0:c0 + H * d].rearrange("p (h i two) -> p h i two", h=H, two=2)
                o = kr[:, c0:c0 + H * d].rearrange("p (h i two) -> p h i two", h=H, two=2)
                x0, x1 = v[:, :, :, 0], v[:, :, :, 1]
                cos = rp[:, ro:ro + hd].unsqueeze(1).to_broadcast([128, H, hd])
                sin = rp[:, ro + hd:ro + 2 * hd].unsqueeze(1).to_broadcast([128, H, hd])
                a1 = rt1[:, 0:H * hd].rearrange("p (h i) -> p h i", h=H)
                a2 = rt2[:, 0:H * hd].rearrange("p (h i) -> p h i", h=H)
                tt(a1, x0, cos, ALU.mult, [rkn, rrp], [rrt])
                tt(a2, x1, sin, ALU.mult, [rkn, rrp], [rrt])
                tt(o[:, :, :, 0], a1, a2, ALU.subtract, [rrt], [rkr])
                tt(a1, x0, sin, ALU.mult, [rkn, rrp], [rrt])
                tt(a2, x1, cos, ALU.mult, [rkn, rrp], [rrt])
                tt(o[:, :, :, 1], a1, a2, ALU.add, [rrt], [rkr])

        r_hn = [Res(f"hn{i}") for i in range(NT)]
        r_loc = [[Res(f"loc{i}_{k}") for k in range(7)] for i in range(NT_OWN)]
        r_g = Res("kv_g")
        r_kx = [[Res(f"kx{i}_{k}") for k in range(3)] for i in range(2)]
        r_kx_all = [r for rr in r_kx for r in rr]
        r_loc_all = [r for rr in r_loc for r in rr]
        r_ktl = [Res(f"ktl{i}") for i in range(9)]
        r_val = [Res(f"val{i}") for i in range(66)]
        rAF = Res("arena_fence")
        ARENA_ALL = ARENA_ALL + r_ktl + r_val

        def arena_fence():
            P.op("vector", lambda e: e.memset(fence_t[:, 0:4], 0.0), (), ARENA_ALL + [rAF])
        r_o = [[Res(f"o{h}_{c}") for c in range(17)] for h in range(12)]
        r_x1 = [Res(f"x1_{i}") for i in range(NT)]
        s_cc = P.new_sem("s_cc")
        s_k = P.new_sem("s_kload")
        s_v = P.new_sem("s_vload")
        out_entries = []

        def x_src(l, i):
            if i < NT_OWN:
                base = x_in if l == 0 else x1_d
                return base[i * 128:(i + 1) * 128, :]
            base = ctx_in if l == 0 else ctx1_d
            return base[(i - NT_OWN) * 128:(i - NT_OWN + 1) * 128, :]

        def x_dst(l, i):
            if i < NT_OWN:
                base = x1_d if l < n_layers - 1 else out_d
                return base[i * 128:(i + 1) * 128, :]
            return ctx1_d[(i - NT_OWN) * 128:(i - NT_OWN + 1) * 128, :]

        def main_body():
          for l in range(n_layers):
              last = (l == n_layers - 1)
              so_ = l * NS
              if l > 0:
                  set_phase("A")
              mod_pieces(l, 8, 12, False, True)
              res32 = cur["res32"]
              dma("sync", res32[0:2, :], bmod_in[l, 2048:3072].partition_broadcast(2), P.new_sem(), [], [rres])
              for hh in range(2):
                  tt(res32[0:2, hh * 512:(hh + 1) * 512], S[0][0:2, hh * 512:(hh + 1) * 512],
                     res32[0:2, hh * 512:(hh + 1) * 512], ALU.add, [rS[0], rres], [rres])
              for s_ in range(2):
                  for hh in range(2):
                      mm(U[1][:, 0:512], sel[0:2, s_, :], res32[0:2, hh * 512:(hh + 1) * 512], True, True,
                         [rres, rconst], [rU[1]])
                      cp(gate_bc[:, s_, hh * 512:(hh + 1) * 512], U[1][:, 0:512], [rU[1]], [rgate])

              if stop == 0:
                  raise _Stop
              arena_fence()
              load_cast(wKV, win_in[l, :, C_KV:C_KV + 768], 768, [rWKV], extra_r=[rAF])
              for i in range(NT):
                  s_ = 0 if i < NT_OWN else 1
                  xt, rxt, sxt = cur["X"].next()
                  dma("sync", xt, x_src(l, i), sxt, [r_x1[i]] if l > 0 else [], [rxt])
                  if s_ == 0:
                      rp, rrp, srp = cur["RP"].next()
                      dma("sync", rp, rope_in[i], srp, [], [rrp])
                  else:
                      rp, rrp = None, None
                  sq = cur["sq"]
                  act(sq[:, 0:D], xt, AF.Square, [rxt], [rsq])
                  red(ssx[:, 0:1], sq[:, 0:D], [rsq], [rssx])
                  rstd_inplace(ssx[:, 0:1], D, rssx)
                  xn, rxn, _ = XN.next()
                  act(xn, xt, AF.Identity, [rxt, rssx], [rxn], scale=ssx[:, 0:1])
                  if stop == 0.1:
                      raise _Stop
                  for fc in range(8):
                      tr(Ubf[0][:, fc * 128:(fc + 1) * 128], xn[:, fc * 128:(fc + 1) * 128], [rxn], [rU[0]])
                  hT, rhT, shT = HT.next()
                  tt(hT, Ubf[0][:, 0:1024].rearrange("p (f t) -> p f t", f=8),
                     gmod[:, l, s_, :].unsqueeze(2).to_broadcast([128, 8, 128]), ALU.mult, [rU[0], rmod], [rhT])
                  tt(hT, hT, shf[:, l, s_, :].unsqueeze(2).to_broadcast([128, 8, 128]), ALU.add, [rhT, rmod], [rhT])
                  dma("gpsimd", hn_d[i], hT, shT, [rhT], [r_hn[i]])
                  if stop == 0.2:
                      raise _Stop
                  for (c0, c1) in ((0, 512), (512, 768)):
                      for fc in range(8):
                          mm(S[0][:, c0:c1], hT[:, fc, :], wKV[:, fc, c0:c1], fc == 0, fc == 7, [rhT, rWKV], [rS[0]])
                  groups = [(0, 2, 64, so_ + S_GKA, 0), (128, 8, 32, so_ + S_GKC, 64)]
                  norm_rope(S[0][:, 0:384], 384, groups, rS[0], rp, rrp, rope=(s_ == 0))
                  if stop == 0.3:
                      raise _Stop
                  kr = cur["kr"]
                  vb, rvb, svb = VB.next()
                  act(vb[:, 0:128], S[0][:, 384:512], AF.Copy, [rS[0]], [rvb])
                  act(vb[:, 128:384], S[0][:, 512:768], AF.Copy, [rS[0]], [rvb])
                  for j in range(3):
                      tr(Ubf[1][:, j * 128:(j + 1) * 128], kr[:, j * 128:(j + 1) * 128], [rkr], [rU[1]])
                  kts, rkts, skts = KTS.next()
                  cp(kts, Ubf[1][:, 0:384].rearrange("p (j t) -> p j t", j=3), [rU[1]], [rkts])
                  if s_ == 0:
                      cols = slice(i * 128, (i + 1) * 128)
                      for j3 in range(3):
                          for half in range(2):
                              dma("gpsimd", lock_t[j3 * 2 + half].ap()[:, cols], kts[half * 64:(half + 1) * 64, j3, :],
                                  skts, [rkts], [r_loc[i][j3 * 2 + half]])
                      dma("gpsimd", vview(locv_t[i // 4].ap())[(i % 4) * 128:(i % 4 + 1) * 128, :], vb, svb, [rvb],
                          [r_loc[i][6]])
                  else:
                      ci = i - NT_OWN
                      cols = slice(ci * 128, (ci + 1) * 128)
                      dma("gpsimd", kx_d[0:128, cols], kts[:, 0, :], skts, [rkts], [r_kx[ci][0]])
                      dma("gpsimd", kx_d[128:384, cols].rearrange("(c p) t -> p c t", p=128), kts[:, 1:3, :], skts,
                          [rkts], [r_kx[ci][1]])
                      dma("gpsimd", vx_d[ci * 128:(ci + 1) * 128, :], vb, svb, [rvb], [r_kx[ci][2]])
                  if stop is not None and 0.4 <= stop < 0.5 and i == round((stop - 0.4) * 1000):
                      raise _Stop
              if stop == 1:
                  raise _Stop
              def allgather(src_t, dst_t):
                  P.dma("gpsimd", lambda e: e.collective_compute("AllGather", ALU.bypass,
                                                                 replica_groups=[[0, 1, 2, 3], [4, 5, 6, 7]],
                                                                 ins=[src_t.ap().opt()], outs=[dst_t.ap().opt()]),
                        s_cc, reads=r_loc_all, writes=[r_g], inc=1)

              for c_ in range(6):
                  allgather(lock_t[c_], gk_t[c_])
              for c_ in range(8):
                  allgather(locv_t[c_], gv_t[c_])

              if stop == 2:
                  raise _Stop
              chunks = [(c * 4, 4, False, c * 512, [(2 * p_, 2 * p_ + 1) for p_ in range(65)]) for c in range(8)]
              if not last:
                  chunks.append((NT_OWN, 2, True, OWN, [(128, 129)]))

              set_phase("T")
              for pname in ("A", "C0", "C1"):
                  if stop == 3 and pname == "C0":
                      raise _Stop
                  if pname == "A":
                      W, qc0, krow, vcol = 512, C_QA, 0, 0
                      groups = [(0, 8, 64, so_ + S_GQA, 0)]
                      streams = [(hk * 64, 64, g, hk, hk * 4 + g) for g in range(4) for hk in range(2)]
                      scale = 64 ** -0.5
                  else:
                      ci = int(pname[1])
                      W, qc0, krow, vcol = 128, C_QC + ci * 128, 128 + ci * 128, 128 + ci * 128
                      groups = [(0, 4, 32, so_ + S_GQC, 64)]
                      streams = [((hl * 2 + m) * 32, 32, 0, hl, 8 + ci * 2 + hl) for hl in range(2) for m in range(2)]
                      scale = 32 ** -0.5
                  nj = W // 128
                  load_cast(wS[:, :, 0:W], win_in[l, :, qc0:qc0 + W], W, [rWS])
                  arena_fence()
                  if pname == "A":
                      for k0 in range(0, NKT, 26):
                          mset(VA[:, k0:k0 + 26, :, 64:128], 1.0, r_val)
                  kc0 = krow // 64
                  for r_ in range(4):
                      for half in range(2):
                          dma("gpsimd", KT[half * 64:(half + 1) * 64, r_ * OWN:(r_ + 1) * OWN],
                              gk_t[kc0 + half].ap()[r_ * 64:(r_ + 1) * 64, :], s_k, [r_g, rAF], [r_ktl[r_ * 2 + half]])
                  dma("gpsimd", KT[:, SEQ:SEQ + CTX], kx_d[krow:krow + 128, :], s_k, r_kx_all + [rAF], [r_ktl[8]])
                  for r_ in range(4):
                      for c_ in range(8):
                          vsrc = vview(gv_t[c_].ap()[r_ * 48:(r_ + 1) * 48, :])
                          for h_ in range(2):
                              dma("gpsimd", VA[:, r_ * 32 + c_ * 4:r_ * 32 + c_ * 4 + 4, h_, 0:64],
                                  vsrc[:, vcol + h_ * 64:vcol + (h_ + 1) * 64].rearrange("(i p) d -> p i d", p=128),
                                  s_v, [r_g, rAF], [r_val[(r_ * 8 + c_) * 2 + h_]])
                  for h_ in range(2):
                      dma("gpsimd", VA[:, 128:130, h_, 0:64],
                          vx_d[:, vcol + h_ * 64:vcol + (h_ + 1) * 64].rearrange("(i p) d -> p i d", p=128), s_v,
                          r_kx_all + [rAF], [r_val[64 + h_]])

                  acc_i = 0
                  s_i = 0
                  for (t0, ntl, is_ctx, ocol, pairs) in chunks:
                      ncol = ntl * 128
                      hn, rhn, shn = cur["HN"].next()
                      dma("sync", hn[:, 0:ntl], hn_d[t0:t0 + ntl].rearrange("i p f t -> p i f t"), shn,
                          r_hn[t0:t0 + ntl], [rhn])
                      qT, rqT, _ = QT.next()
                      for ti in range(ntl):
                          if not is_ctx:
                              rp, rrp, srp = cur["RP"].next()
                              dma("sync", rp, rope_in[t0 + ti], srp, [], [rrp])
                          else:
                              rp, rrp = None, None
                          for fc in range(8):
                              mm(U[0][:, 0:W], hn[:, ti, fc, :], wS[:, fc, 0:W], fc == 0, fc == 7, [rhn, rWS], [rU[0]])
                          norm_rope(U[0][:, 0:W], W, groups, rU[0], rp, rrp, rope=not is_ctx)
                          kr = cur["kr"]
                          for j in range(nj):
                              tr(Ubf[1][:, j * 128:(j + 1) * 128], kr[:, j * 128:(j + 1) * 128], [rkr], [rU[1]])
                          cp(qT[:, 0:nj, ti * 128:(ti + 1) * 128], Ubf[1][:, 0:nj * 128].rearrange("p (j t) -> p j t", j=nj),
                             [rU[1]], [rqT])
                      for si, (prow, kd, qj, vh, ohead) in enumerate(streams):
                          acc, racc = ACC[acc_i % 2], rACC[acc_i % 2]
                          acc_i += 1
                          tp = (prow, 0) if prow == 96 else None
                          npairs = len(pairs)

                          def qk(pi, s_base=s_i, prow=prow, kd=kd, qj=qj, tp=tp, pairs=pairs, ncol=ncol, qT=qT, rqT=rqT):
                              Sx, rSx = S[(s_base + pi) % 2], rS[(s_base + pi) % 2]
                              for hh, kt in enumerate(pairs[pi]):
                                  mm(Sx[:, hh * 512:hh * 512 + ncol], KT[prow:prow + kd, kt * 128:(kt + 1) * 128],
                                     qT[prow:prow + kd, qj, 0:ncol], True, True, r_ktl + [rqT], [rSx], tp=tp)

                          qk(0)
                          for pi in range(npairs):
                              if pi + 1 < npairs:
                                  qk(pi + 1)
                              Sx, rSx = S[(s_i + pi) % 2], rS[(s_i + pi) % 2]
                              pt, rpt, _ = PT.next()
                              for hh in range(2):
                                  act(pt[:, hh * 512:hh * 512 + ncol], Sx[:, hh * 512:hh * 512 + ncol], AF.Exp, [rSx],
                                      [rpt], scale=scale)
                              for hh, kt in enumerate(pairs[pi]):
                                  mm(acc[:, 0:ncol], VA[:, kt, vh, :], pt[:, hh * 512:hh * 512 + ncol],
                                     pi == 0 and hh == 0, pi == npairs - 1 and hh == 1, r_val + [rpt], [racc])
                          s_i += npairs
                          rcp(R[64:128, 0:ncol], acc[64:128, 0:ncol], [racc], [rR])
                          if pname == "A":
                              onb, ronb, sonb = ONB.next()
                              tt(onb[0:64, 0:ncol], acc[0:64, 0:ncol], R[64:128, 0:ncol], ALU.mult, [racc, rR], [ronb])
                              dma("gpsimd", o_d[:, ohead, ocol:ocol + ncol], onb[0:64, 0:ncol], sonb, [ronb],
                                  [r_o[ohead][ocol // 256], r_o[ohead][ocol // 256 + (1 if ncol == 512 else 0)]])
                          else:
                              m_ = si % 2
                              tt(on32[0:64, m_, 0:ncol], acc[0:64, 0:ncol], R[64:128, 0:ncol], ALU.mult, [racc, rR],
                                 [ron[m_]])
                              if m_ == 1:
                                  stt(dif[0:64, 0:ncol], on32[0:64, 1, 0:ncol], lamw[0:64, l, 3:4], on32[0:64, 0, 0:ncol],
                                      ALU.mult, ALU.add, [ron[0], ron[1], rmod], [rdif])
                                  tt(dsq[0:64, 0:ncol], dif[0:64, 0:ncol], dif[0:64, 0:ncol], ALU.mult, [rdif], [rdsq])
                                  mm(U[0][0:64, 0:ncol], ones64[:, :], dsq[0:64, 0:ncol], True, True, [rdsq, rconst],
                                     [rU[0]])
                                  act(R[0:64, 0:ncol], U[0][0:64, 0:ncol], AF.Ln, [rU[0]], [rR], scale=1.0 / 64, bias=EPS)
                                  act(R[0:64, 0:ncol], R[0:64, 0:ncol], AF.Exp, [rR], [rR], scale=-0.5)
                                  tt(dif[0:64, 0:ncol], dif[0:64, 0:ncol], R[0:64, 0:ncol], ALU.mult, [rdif, rR], [rdif])
                                  onb, ronb, sonb = ONB.next()
                                  ts(onb[0:64, 0:ncol], dif[0:64, 0:ncol], lamw[0:64, l, 4:5], ALU.mult, [rdif, rmod],
                                     [ronb])
                                  dma("gpsimd", o_d[:, ohead, ocol:ocol + ncol], onb[0:64, 0:ncol], sonb, [ronb],
                                      [r_o[ohead][ocol // 256], r_o[ohead][ocol // 256 + (1 if ncol == 512 else 0)]])

              if stop == 4:
                  raise _Stop
              set_phase("M")
              arena_fence()
              load_cast(wM, win_in[l, :, C_M:C_M + 4352], 4352, [rWM], extra_r=[rAF])
              load_cast(wBV, win_in[l, :, C_BV:C_BV + 256], 256, [rWM], extra_r=[rAF])
              load_cast(wOUT, wout_in[l], 1024, [rWM], extra_r=[rAF])
              load_cast(wPA, wpa_in[l], 1024, [rWM], extra_r=[rAF], kdim=4)
              load_cast(wPC, wpc_in[l], 1024, [rWM], extra_r=[rAF], kdim=2)
              load_cast(wPB, wpb_in[l], 1024, [rWM], extra_r=[rAF], nparts=64, kdim=4)
              stg, rstg, sstg = cur["STG"].next()
              sv = stg[:, 0:512].rearrange("p (g q) -> p g q", g=4)
              dma("sync", sv, wsp_in[l], sstg, [], [rstg])
              cp(wsp, sv, [rstg], [rwsp])
              bsp_bc = small[0:64, so_ + S_BSP:so_ + S_BSP + 512].rearrange("p (g t) -> p g t", g=4)
              gsgu_bc = small[:, so_ + S_SGU:so_ + S_SGU + 256]
              sq, kn, res32 = cur["sq"], cur["kn"], cur["res32"]

              def sigmoid_from(ps_ap, r_ps, eb, rebx):
                  act(eb, ps_ap, AF.Exp, [r_ps], [rebx], scale=-1.0)
                  ts(eb, eb, 1.0, ALU.add, [rebx], [rebx])
                  rcp(eb, eb, [rebx], [rebx])

              mchunks = [(h * 2, 2, False, h * 256) for h in range(16)]
              if not last:
                  mchunks.append((NT_OWN, 2, True, OWN))
              for (t0, ntl, is_ctx, ocol) in mchunks:
                  ncol = NM
                  s_ = 1 if is_ctx else 0
                  hn, rhn, shn = cur["HN"].next()
                  dma("sync", hn, hn_d[t0:t0 + ntl].rearrange("i p f t -> p i f t"), shn, r_hn[t0:t0 + ntl], [rhn])
                  og, rog, sog = OG.next()
                  for par in range(2):
                      dma("sync", og[par * 64:(par + 1) * 64, :, :],
                          o_d[:, :, ocol:ocol + ncol].rearrange("d (c two) t -> d two c t", two=2)[:, par],
                          sog, [r_o[h][ocol // 256] for h in range(12)], [rog])

                  def proj(ps_ap, col0, ncols_w, r_ps):
                      for fc in range(8):
                          mm(ps_ap, wM[:, fc, col0:col0 + ncols_w], hn[:, :, fc, :], fc == 0, fc == 7, [rWM, rhn], [r_ps])

                  for ti in range(ntl):
                      for fc in range(8):
                          mm(U[0][:, 0:256], hn[:, ti, fc, :], wBV[:, fc, :], fc == 0, fc == 7, [rhn, rWM], [rU[0]])
                      act(sq[:, 0:256], U[0][:, 0:256], AF.Square, [rU[0]], [rsq])
                      red(ss[:, 0:4], sq[:, 0:256].rearrange("p (h d) -> p h d", h=4), [rsq], [rss])
                      rstd_inplace(ss[:, 0:4], 64, rss)
                      tt(kn[:, 0:256].rearrange("p (h d) -> p h d", h=4), U[0][:, 0:256].rearrange("p (h d) -> p h d", h=4),
                         ss[:, 0:4].unsqueeze(2).to_broadcast([128, 4, 64]), ALU.mult, [rU[0], rss], [rkn])
                      tt(vn, kn[:, 0:256], gsgu_bc, ALU.mult, [rkn, rconst], [rvn])
                      for g in range(4):
                          mm(U[1][0:64, g * 128:(g + 1) * 128], vn[:, g * 64:(g + 1) * 64], wsp[:, g, :], True, True,
                             [rvn, rwsp], [rU[1]])
                      tt(mixsb[0:64, :, ti * 128:(ti + 1) * 128], U[1][0:64, 0:512].rearrange("p (g t) -> p g t", g=4),
                         bsp_bc, ALU.add, [rU[1], rconst], [rmix])

                  zi = 0
                  for blk in range(6):
                      zps, rz = ACC[zi % 2], rACC[zi % 2]
                      col0 = blk * 128 if blk < 4 else 768 + (blk - 4) * 128
                      proj(zps[:, 0:ncol], col0, 128, rz)
                      eb, tb = ebuf[:, zi % 2, :], tbuf[:, zi % 2, :]
                      sigmoid_from(zps[:, 0:ncol], rz, eb, reb[zi % 2])
                      tt(tb, zps[:, 0:ncol], eb, ALU.mult, [rz, reb[zi % 2]], [rtb[zi % 2]])
                      tt(G_AC[:, blk, :], tb, og[:, blk, :], ALU.mult, [rtb[zi % 2], rog], [rG])
                      zi += 1
                  for g in range(4):
                      zps, rz = ACC[zi % 2], rACC[zi % 2]
                      proj(zps[0:64, 0:ncol], 512 + g * 64, 64, rz)
                      eb, tb = ebuf[0:64, zi % 2, :], tbuf[0:64, zi % 2, :]
                      sigmoid_from(zps[0:64, 0:ncol], rz, eb, reb[zi % 2])
                      tt(tb, zps[0:64, 0:ncol], eb, ALU.mult, [rz, reb[zi % 2]], [rtb[zi % 2]])
                      ups, ru = U[g % 2], rU[g % 2]
                      proj(ups[0:64, 0:ncol], 1024 + g * 64, 64, ru)
                      tt(eb, ups[0:64, 0:ncol], mixsb[0:64, g, :], ALU.mult, [ru, rmix], [reb[zi % 2]])
                      tt(G_B[0:64, g, :], tb, eb, ALU.mult, [rtb[zi % 2], reb[zi % 2]], [rG])
                      zi += 1

                  for n in range(8):
                      nb = slice(n * 128, (n + 1) * 128)
                      ysl = [(S[0][:, 0:ncol], rS[0]), (S[0][:, 512:512 + ncol], rS[0]), (S[1][:, 0:ncol], rS[1])]
                      for c_ in range(4):
                          mm(ysl[0][0], wPA[:, c_, nb], G_AC[:, c_, :], c_ == 0, c_ == 3, [rWM, rG], [ysl[0][1]])
                      for g in range(4):
                          mm(ysl[1][0], wPB[0:64, g, nb], G_B[0:64, g, :], g == 0, g == 3, [rWM, rG], [ysl[1][1]])
                      for c_ in range(2):
                          mm(ysl[2][0], wPC[:, c_, nb], G_AC[:, 4 + c_, :], c_ == 0, c_ == 1, [rWM, rG], [ysl[2][1]])
                      for br in range(3):
                          gi = (n * 3 + br) % 2
                          gps, rg_ = ACC[gi], rACC[gi]
                          proj(gps[:, 0:ncol], 1280 + br * 1024 + n * 128, 128, rg_)
                          sigmoid_from(gps[:, 0:ncol], rg_, sig[:, br, :], rsig[br])
                      tt(macc, ysl[0][0], sig[:, 0, :], ALU.mult, [ysl[0][1], rsig[0]], [rmacc])
                      tt(tbuf[:, 0, :], ysl[1][0], sig[:, 1, :], ALU.mult, [ysl[1][1], rsig[1]], [rtb[0]])
                      tt(macc, macc, tbuf[:, 0, :], ALU.add, [rmacc, rtb[0]], [rmacc])
                      tt(tbuf[:, 1, :], ysl[2][0], sig[:, 2, :], ALU.mult, [ysl[2][1], rsig[2]], [rtb[1]])
                      tt(mT[:, n, :], macc, tbuf[:, 1, :], ALU.add, [rmacc, rtb[1]], [rmT])

                  for ti in range(ntl):
                      i = t0 + ti
                      xt, rxt, sxt = cur["X"].next()
                      dma("sync", xt, x_src(l, i), sxt, [r_x1[i]] if l > 0 else [], [rxt])
                      for hh in range(2):
                          for k_ in range(8):
                              mm(S[1][:, hh * 512:(hh + 1) * 512], mT[:, k_, ti * 128:(ti + 1) * 128],
                                 wOUT[:, k_, hh * 512:(hh + 1) * 512], k_ == 0, k_ == 7, [rmT, rWM], [rS[1]])
                      for hh in range(2):
                          tt(res32[:, hh * 512:(hh + 1) * 512], S[1][:, hh * 512:(hh + 1) * 512],
                             gate_bc[:, s_, hh * 512:(hh + 1) * 512], ALU.mult, [rS[1], rgate], [rres])
                      tt(xt, xt, res32, ALU.add, [rxt, rres], [rxt])
                      e = dma("gpsimd", x_dst(l, i), xt, sxt, [rxt], [r_x1[i]])
                      if last:
                          out_entries.append(e)


        try:
            main_body()
        except _Stop:
            pass
        if not out_entries:
            out_entries.append(dma("sync", out_d[0:128, 0:16], ss[:, 0:16], P.new_sem(), [rss], []))
        for e in out_entries:
            P.wait_at_end(e)
        stats = P.emit()
    return nc, stats


def _rope_tables(j):
    n = j * OWN + np.arange(OWN)
    rows = (n // 64).astype(np.float32)
    cols = (n % 64).astype(np.float32)
    out = np.zeros((OWN, 96), np.float32)

    def tab(dim):
        quarter = dim // 4
        inv = np.power(np.float32(10000.0), -np.arange(quarter, dtype=np.float32) / quarter).astype(np.float32)
        ang = np.concatenate([rows[:, None] * inv, cols[:, None] * inv], axis=-1).astype(np.float32)
        return np.cos(ang).astype(np.float32), np.sin(ang).astype(np.float32)

    ca, sa = tab(64)
    cq, sq_ = tab(32)
    out[:, 0:32], out[:, 32:64], out[:, 64:80], out[:, 80:96] = ca, sa, cq, sq_
    return np.ascontiguousarray(out.reshape(NT_OWN, 128, 96))


def _perm_cols():
    r = np.arange
    qa = np.array([(hk * 4 + g) * 64 + d for g in range(4) for hk in range(2) for d in range(64)])
    return np.concatenate([r(768, 896), r(1024, 1280), r(896, 1024), r(1280, 1536), qa, r(512, 768),
                           r(1536, 2048), r(2816, 3072), r(2048, 2304), r(2304, 2560), r(3072, 6144), r(2560, 2816)])


_CACHE = {}


def kernel(x, c, ctx, c_ctx, w_mod, b_mod, g_norm, w_in, gq_a, gk_a, gq_c, gk_c,
           g_sgu, w_sp, b_sp, lam_c, g_subln, w_pa, w_pb, w_pc, w_out):
    f = lambda a: np.ascontiguousarray(np.asarray(a, dtype=np.float32))
    x, c, ctx, c_ctx = f(x), f(c), f(ctx), f(c_ctx)
    w_mod, b_mod, g_norm, w_in = f(w_mod), f(b_mod), f(g_norm), f(w_in)
    L = w_in.shape[0]
    if "nc" not in _CACHE:
        _CACHE["nc"] = build_program(L)
    nc, _ = _CACHE["nc"]
    w_in_p = np.ascontiguousarray(w_in[:, :, _perm_cols()])
    small = np.concatenate([f(gq_a), f(gk_a), f(gq_c), f(gk_c), f(g_sgu).reshape(L, 256),
                            f(b_sp).reshape(L, 512)], axis=1)
    shared = {
        "w_mod": w_mod,
        "bmT": np.ascontiguousarray(b_mod.reshape(L, 24, 128).transpose(0, 2, 1)),
        "b_mod": b_mod,
        "gn": np.ascontiguousarray(g_norm.reshape(L, 8, 128).transpose(0, 2, 1)),
        "w_in": w_in_p,
        "small": np.ascontiguousarray(small),
        "gsub": np.ascontiguousarray(np.tile(f(g_subln), (1, 2)).T),
        "lam": np.ascontiguousarray(f(lam_c).reshape(L, 128)),
        "w_spT": np.ascontiguousarray(f(w_sp).transpose(0, 3, 1, 2)),
        "w_pa": f(w_pa),
        "w_pb": f(w_pb),
        "w_pc": f(w_pc),
        "w_out": f(w_out),
        "ident": np.eye(128, dtype=np.float32).astype(ml_dtypes.bfloat16),
        "sel": np.ascontiguousarray(np.stack([np.stack([np.ones(128), np.zeros(128)]),
                                              np.stack([np.zeros(128), np.ones(128)])], axis=1).astype(np.float32)),
    }
    in_maps = []
    for core in range(NCORES):
        b, j = core // 4, core % 4
        cc2 = np.stack([c[b], c_ctx], axis=-1).reshape(8, 128, 2).transpose(1, 0, 2).reshape(128, 16)
        m = dict(shared)
        m["x"] = np.ascontiguousarray(x[b, j * OWN:(j + 1) * OWN])
        m["ctx"] = np.ascontiguousarray(ctx[b])
        m["cc"] = np.ascontiguousarray(cc2)
        m["rope"] = _rope_tables(j)
        in_maps.append(m)
    res = run_bass_kernel_spmd(nc, in_maps, core_ids=list(range(NCORES)))
    out = np.empty((2, SEQ, D), np.float32)
    for core in range(NCORES):
        b, j = core // 4, core % 4
        out[b, j * OWN:(j + 1) * OWN] = res.results[core]["out"]
    return out
```

```python
import contextlib
import math
import numpy as np
import ml_dtypes
import concourse.bass as bass
import concourse.mybir as mybir
from concourse.bass_utils import run_bass_kernel_spmd

F32 = mybir.dt.float32
BF16 = mybir.dt.bfloat16
ALU = mybir.AluOpType
AF = mybir.ActivationFunctionType
AX = mybir.AxisListType

NCORES = 8
D = 1024
SEQ = 16384
OWN = 4096
CTX = 256
NT_OWN = 32
NT = 34
EPS = 1e-6
NKT = 130

C_KV = 0
C_QA = 768
C_QC = 1280
C_M = 1536
C_BV = 5888
S_GQA, S_GKA, S_GQC, S_GKC, S_SGU, S_BSP, NS = 0, 64, 128, 160, 192, 448, 960

COMPUTE = ("tensor", "vector", "scalar", "gpsimd")
QUEUES = ("sync", "tensor", "vector", "scalar", "gpsimd")


class Res:
    __slots__ = ("name", "last_w", "readers", "excl")

    def __init__(self, name="", excl=False):
        self.name = name
        self.last_w = None
        self.readers = {}
        self.excl = excl


class Entry:
    __slots__ = ("eng", "fn", "waits", "signal", "val", "sem", "inc", "kind")

    def __init__(self, eng, fn, kind):
        self.eng = eng
        self.fn = fn
        self.waits = []
        self.signal = False
        self.val = None
        self.sem = None
        self.inc = 0
        self.kind = kind


class Prog:
    def __init__(self, nc, stack, same_engine_sync=True):
        self.nc = nc
        self.stack = stack
        self.q = {e: [] for e in QUEUES}
        self.esem = {e: stack.enter_context(nc.semaphore("es_" + e)) for e in COMPUTE}
        self.same = same_engine_sync
        self.nsem = 0
        self.final = []

    def new_sem(self, name=None):
        self.nsem += 1
        return [self.stack.enter_context(self.nc.semaphore(name or f"ds{self.nsem}")), 0]

    def _deps(self, e, reads, writes):
        ex = [r for r in reads if r.excl]
        if ex:
            reads = [r for r in reads if not r.excl]
            writes = list(writes) + ex
        deps = []
        for r in reads:
            if r.last_w is not None:
                deps.append(r.last_w)
        for w in writes:
            if w.last_w is not None:
                deps.append(w.last_w)
            deps.extend(w.readers.values())
        seen = set()
        for d in deps:
            if d is e or id(d) in seen:
                continue
            seen.add(id(d))
            if d.kind == "c":
                if d.eng == e.eng and e.kind == "c" and (d.eng == "tensor" or not self.same):
                    continue
                d.signal = True
            e.waits.append(d)
        for r in reads:
            key = e.eng if e.kind == "c" else id(e)
            r.readers[key] = e
        for w in writes:
            w.last_w = e
            w.readers = {}

    def op(self, eng, fn, reads=(), writes=()):
        e = Entry(eng, fn, "c")
        self._deps(e, reads, writes)
        self.q[eng].append(e)
        return e

    def dma(self, eng, fn, sem, reads=(), writes=(), inc=16):
        e = Entry(eng, fn, "d")
        self._deps(e, reads, writes)
        sem[1] += inc
        e.sem = sem[0]
        e.inc = inc
        e.val = sem[1]
        self.q[eng].append(e)
        return e

    def wait_at_end(self, e):
        if e.kind == "c":
            e.signal = True
        self.final.append(e)

    def emit(self):
        for eng in COMPUTE:
            c = 0
            for e in self.q[eng]:
                if e.kind == "c" and e.signal:
                    c += 1
                    e.val = c
        stats = {}
        with self.nc.Block() as block:
            for eng in QUEUES:
                entries = self.q[eng]
                if not entries and eng != "sync":
                    continue
                stats[eng] = len(entries)

                def body(engine, entries=entries, eng=eng):
                    waited = {}

                    def wait(d):
                        sem = self.esem[d.eng] if d.kind == "c" else d.sem
                        k = id(sem)
                        if waited.get(k, 0) >= d.val:
                            return
                        waited[k] = d.val
                        engine.wait_ge(sem, d.val)

                    for e in entries:
                        for d in e.waits:
                            wait(d)
                        ins = e.fn(engine)
                        if e.kind == "c":
                            if e.signal:
                                ins.then_inc(self.esem[eng], 1)
                        else:
                            ins.then_inc(e.sem, e.inc)
                    if eng == "sync":
                        for d in self.final:
                            wait(d)

                getattr(block, eng)(body)
        return stats


class _Stop(Exception):
    pass


def build_program(n_layers=2, stop=None):
    nc = bass.Bass("TRN2", target_bir_lowering=False)
    dt = lambda name, shape, dtp, kind=None: (nc.dram_tensor(name, shape, dtp, kind=kind) if kind
                                              else nc.dram_tensor(name, shape, dtp))
    x_in = dt("x", [OWN, D], F32, "ExternalInput").ap()
    ctx_in = dt("ctx", [CTX, D], F32, "ExternalInput").ap()
    cc_in = dt("cc", [128, 16], F32, "ExternalInput").ap()
    wmod_in = dt("w_mod", [2, D, 3 * D], F32, "ExternalInput").ap()
    bmT_in = dt("bmT", [2, 128, 24], F32, "ExternalInput").ap()
    bmod_in = dt("b_mod", [2, 3 * D], F32, "ExternalInput").ap()
    gn_in = dt("gn", [2, 128, 8], F32, "ExternalInput").ap()
    win_in = dt("w_in", [2, D, 6144], F32, "ExternalInput").ap()
    small_in = dt("small", [2, NS], F32, "ExternalInput").ap()
    gsub_in = dt("gsub", [128, 2], F32, "ExternalInput").ap()
    lam_in = dt("lam", [2, 128], F32, "ExternalInput").ap()
    wsp_in = dt("w_spT", [2, 128, 4, 128], F32, "ExternalInput").ap()
    wpa_in = dt("w_pa", [2, 512, D], F32, "ExternalInput").ap()
    wpb_in = dt("w_pb", [2, 256, D], F32, "ExternalInput").ap()
    wpc_in = dt("w_pc", [2, 256, D], F32, "ExternalInput").ap()
    wout_in = dt("w_out", [2, D, D], F32, "ExternalInput").ap()
    rope_in = dt("rope", [NT_OWN, 128, 96], F32, "ExternalInput").ap()
    ident_in = dt("ident", [128, 128], BF16, "ExternalInput").ap()
    sel_in = dt("sel", [2, 2, 128], F32, "ExternalInput").ap()
    out_d = dt("out", [OWN, D], F32, "ExternalOutput").ap()

    x1_d = dt("x1", [OWN, D], F32).ap()
    ctx1_d = dt("ctx1", [CTX, D], F32).ap()
    hn_d = dt("hn_scr", [NT, 128, 8, 128], BF16).ap()
    o_d = dt("o_scr", [64, 12, OWN + CTX], BF16).ap()
    lock_t = [dt(f"lock{c}", [64, OWN], BF16) for c in range(6)]
    gk_t = [dt(f"gk{c}", [256, OWN], BF16) for c in range(6)]
    locv_t = [dt(f"locv{c}", [48, OWN], BF16) for c in range(8)]
    gv_t = [dt(f"gv{c}", [192, OWN], BF16) for c in range(8)]
    kx_d = dt("kx", [384, CTX], BF16).ap()
    vx_d = dt("vx", [CTX, 384], BF16).ap()

    def vview(ap2d):
        return ap2d.rearrange("r c -> (r c)").rearrange("(t f) -> t f", f=384)

    with contextlib.ExitStack() as st:
        P = Prog(nc, st)
        sb = lambda n, s, d: st.enter_context(nc.sbuf_tensor("sb_" + n, s, d))
        pst = lambda n, s, d: st.enter_context(nc.psum_tensor(n, s, d))

        ARENA_N = 55296
        arena = sb("arena", [128, ARENA_N], BF16)
        KT = arena[:, 0:16640]
        VA = arena[:, 16640:16640 + NKT * 256].rearrange("p (k h c) -> p k h c", k=NKT, h=2)
        wKV = arena[:, 0:6144].rearrange("p (f n) -> p f n", f=8)
        wM = arena[:, 0:8 * 4352].rearrange("p (f n) -> p f n", f=8)
        wBV = arena[:, 34816:34816 + 2048].rearrange("p (f n) -> p f n", f=8)
        wOUT = arena[:, 36864:36864 + 8192].rearrange("p (f n) -> p f n", f=8)
        wPA = arena[:, 45056:45056 + 4096].rearrange("p (c n) -> p c n", c=4)
        wPC = arena[:, 49152:49152 + 2048].rearrange("p (c n) -> p c n", c=2)
        wPB = arena[0:64, 51200:51200 + 4096].rearrange("p (c n) -> p c n", c=4)
        rKT, rVA, rWM, rWKV = Res("KT"), Res("VA"), Res("wM"), Res("wKV")
        ARENA_ALL = [rKT, rVA, rWM, rWKV]
        ident = sb("ident", [128, 128], BF16)
        ones64 = sb("ones64", [64, 64], F32)
        sel = sb("sel", [2, 2, 128], F32)
        cc = sb("cc", [128, 16], F32)
        silc = sb("silc", [128, 16], F32)
        small = sb("small", [128, 2 * NS], F32)
        modT = sb("modT", [128, 2, 24, 2], F32)
        bmT = sb("bmT", [128, 2, 24], F32)
        gn = sb("gn", [128, 2, 8], F32)
        gmod = sb("gmod", [128, 2, 2, 8], F32)
        shf = sb("shf", [128, 2, 2, 8], F32)
        gate_bc = sb("gate_bc", [128, 2, 1024], F32)
        lamt = sb("lamt", [128, 2, 128], F32)
        lamw = sb("lamw", [128, 2, 8], F32)
        gsub = sb("gsub", [128, 2], F32)
        ss = sb("ss", [128, 16], F32)
        ssx = sb("ssx", [128, 2], F32)
        rconst, rmod, rgate, rss, rssx = Res("const"), Res("mod"), Res("gate_bc"), Res("ss"), Res("ssx")

        SCR_BYTES = 57344
        scr = sb("scr", [128, SCR_BYTES // 2], BF16)

        class Carver:
            def __init__(self):
                self.off = 0

            def take(self, nfree, dtype, parts=128):
                n2 = nfree * (2 if dtype == F32 else 1)
                n2 = (n2 + 15) // 16 * 16
                assert self.off + n2 <= SCR_BYTES // 2, (self.off, n2)
                ap = scr[0:parts, self.off:self.off + n2]
                self.off += n2
                if dtype == F32:
                    ap = ap.bitcast(F32)
                return ap[:, 0:nfree]

        class Rot:
            def __init__(self, name, aps):
                self.slots = [(a, Res(f"{name}{i}"), P.new_sem()) for i, a in enumerate(aps)]
                self.i = 0

            def next(self):
                s_ = self.slots[self.i % len(self.slots)]
                self.i += 1
                return s_

            def res(self):
                return [s_[1] for s_ in self.slots]

        cA = Carver()
        X_A = Rot("xA", [cA.take(D, F32) for _ in range(2)])
        XN = Rot("xn", [cA.take(D, BF16)])
        HT = Rot("hT", [cA.take(1024, BF16).rearrange("p (f t) -> p f t", f=8) for _ in range(2)])
        STG_A = Rot("stgA", [cA.take(1024, F32) for _ in range(2)])
        RP_A = Rot("rpA", [cA.take(96, F32) for _ in range(2)])
        KTS = Rot("kts", [cA.take(384, BF16).rearrange("p (j t) -> p j t", j=3) for _ in range(2)])
        VB = Rot("vb", [cA.take(384, BF16) for _ in range(2)])
        sq_A, kn_A, kr_A = cA.take(1024, F32), cA.take(512, F32), cA.take(512, BF16)
        rt1_A, rt2_A = cA.take(256, F32), cA.take(256, F32)
        res32_A = cA.take(1024, F32)
        cT = Carver()
        HN_T = Rot("hnT", [cT.take(1024, BF16).rearrange("p (f t) -> p f t", f=8) for _ in range(2)])
        STG_T = Rot("stgT", [cT.take(1024, F32) for _ in range(2)])
        RP_T = Rot("rpT", [cT.take(96, F32) for _ in range(2)])
        wS = cT.take(4096, BF16).rearrange("p (f n) -> p f n", f=8)
        rWS = Res("wS")
        sq_T, kn_T, kr_T = cT.take(512, F32), cT.take(512, F32), cT.take(512, BF16)
        rt1_T, rt2_T = cT.take(256, F32), cT.take(256, F32)
        PT = Rot("pt", [cT.take(1024, BF16) for _ in range(3)])
        QT = Rot("qT", [cT.take(2048, BF16).rearrange("p (j t) -> p j t", j=4) for _ in range(2)])
        ONB = Rot("onb", [cT.take(512, BF16) for _ in range(2)])
        R = cT.take(512, F32)
        on32 = cT.take(1024, F32).rearrange("p (m t) -> p m t", m=2)
        dif, dsq = cT.take(512, F32), cT.take(512, F32)
        rR, ron, rdif, rdsq = Res("R"), [Res("on0"), Res("on1")], Res("dif"), Res("dsq")
        cM = Carver()
        NM = 256
        HN_M = Rot("hnM", [cM.take(2048, BF16).rearrange("p (i f t) -> p i f t", i=2, f=8) for _ in range(2)])
        STG_M = Rot("stgM", [cM.take(1024, F32) for _ in range(2)])
        X_M = Rot("xM", [cM.take(D, F32) for _ in range(2)])
        OG = Rot("og", [cM.take(6 * NM, BF16).rearrange("p (c t) -> p c t", c=6)])
        sq_M, kn_M = cM.take(256, F32), cM.take(256, F32)
        G_AC = cM.take(6 * NM, BF16).rearrange("p (c t) -> p c t", c=6)
        G_B = cM.take(4 * NM, BF16).rearrange("p (c t) -> p c t", c=4)
        mT = cM.take(8 * NM, BF16).rearrange("p (c t) -> p c t", c=8)
        mixsb = cM.take(4 * NM, F32).rearrange("p (c t) -> p c t", c=4)
        sig = cM.take(3 * NM, F32).rearrange("p (c t) -> p c t", c=3)
        ebuf = cM.take(2 * NM, F32).rearrange("p (c t) -> p c t", c=2)
        tbuf = cM.take(2 * NM, F32).rearrange("p (c t) -> p c t", c=2)
        res32_M = cM.take(1024, F32)
        macc = cM.take(NM, F32)
        vn = cM.take(256, BF16)
        wsp = cM.take(512, BF16).rearrange("p (g q) -> p g q", g=4)
        rG, rmT, rmix, rvn, rwsp, rmacc, rres = (Res("G"), Res("mT"), Res("mix"), Res("vn"), Res("wsp"),
                                                 Res("macc"), Res("res32"))
        reb, rtb, rsig = [Res("eb0"), Res("eb1")], [Res("tb0"), Res("tb1")], [Res("sg0"), Res("sg1"), Res("sg2")]
        rsq, rkn, rkr, rrt = Res("sq"), Res("kn"), Res("kr"), Res("rt")
        SCR_RES = ([rsq, rkn, rkr, rrt, rres, rWS, rR, rdif, rdsq, rG, rmT, rmix, rvn, rwsp, rmacc] + ron + reb + rtb
                   + rsig)
        for rot in (X_A, XN, HT, STG_A, RP_A, KTS, VB, HN_T, STG_T, RP_T, PT, QT, ONB, HN_M, STG_M, X_M, OG):
            SCR_RES += rot.res()
        fence_t = sb("fence_t", [128, 8], F32)
        rfence = Res("fence")

        def fence():
            P.op("vector", lambda e: e.memset(fence_t[:], 0.0), (), SCR_RES + [rfence])

        cur = {}

        def set_phase(ph):
            fence()
            if ph == "A":
                cur.update(X=X_A, STG=STG_A, RP=RP_A, sq=sq_A, kn=kn_A, kr=kr_A, rt1=rt1_A, rt2=rt2_A, res32=res32_A)
            elif ph == "T":
                cur.update(HN=HN_T, STG=STG_T, RP=RP_T, sq=sq_T, kn=kn_T, kr=kr_T, rt1=rt1_T, rt2=rt2_T)
            else:
                cur.update(HN=HN_M, STG=STG_M, X=X_M, sq=sq_M, kn=kn_M, res32=res32_M)

        set_phase("A")

        S = [pst("S0", [128, 1024], F32), pst("S1", [128, 1024], F32)]
        rS = [Res("S0", True), Res("S1", True)]
        ACC = [pst("acc0", [128, 512], F32), pst("acc1", [128, 512], F32)]
        rACC = [Res("acc0", True), Res("acc1", True)]
        U = [pst("U0", [128, 512], F32), pst("U1", [128, 512], F32)]
        rU = [Res("U0", True), Res("U1", True)]
        Ubf = [u[:].bitcast(BF16) for u in U]

        def mm(out, lhsT, rhs, start, stop, r, w, tp=None):
            if tp is None:
                return P.op("tensor", lambda e: e.matmul(out, lhsT=lhsT, rhs=rhs, start=start, stop=stop), r, w)
            return P.op("tensor", lambda e: e.matmul(out, lhsT=lhsT, rhs=rhs, start=start, stop=stop,
                                                     tile_position=tp), r, w)

        def tr(out, in_, r, w):
            return P.op("tensor", lambda e: e.transpose(out, in_, ident[:]), list(r) + [rconst], w)

        def act(out, in_, func, r, w, scale=1.0, bias=0.0):
            return P.op("scalar", lambda e: e.activation(out=out, in_=in_, func=func, bias=bias, scale=scale), r, w)

        def tt(out, in0, in1, op, r, w, eng="vector"):
            return P.op(eng, lambda e: e.tensor_tensor(out=out, in0=in0, in1=in1, op=op), r, w)

        def ts(out, in0, s1, op0, r, w, s2=None, op1=None, eng="vector"):
            if op1 is None:
                return P.op(eng, lambda e: e.tensor_scalar(out=out, in0=in0, scalar1=s1, scalar2=None, op0=op0), r, w)
            return P.op(eng, lambda e: e.tensor_scalar(out=out, in0=in0, scalar1=s1, scalar2=s2, op0=op0, op1=op1), r, w)

        def stt(out, in0, scalar, in1, op0, op1, r, w):
            return P.op("vector", lambda e: e.scalar_tensor_tensor(out=out, in0=in0, scalar=scalar, in1=in1,
                                                                   op0=op0, op1=op1), r, w)

        def cp(out, in_, r, w, eng="vector"):
            return P.op(eng, lambda e: e.tensor_copy(out=out, in_=in_), r, w)

        def red(out, in_, r, w):
            return P.op("vector", lambda e: e.tensor_reduce(out=out, in_=in_, axis=AX.X, op=ALU.add), r, w)

        def rcp(out, in_, r, w):
            return P.op("vector", lambda e: e.reciprocal(out=out, in_=in_), r, w)

        def dma(q, out, in_, sem, r, w):
            return P.dma(q, lambda e: e.dma_start(out=out, in_=in_), sem, r, w)

        def mset(ap, val, w):
            return P.op("vector", lambda e: e.memset(ap, val), (), w)

        def rstd_inplace(ap, d, r):
            act(ap, ap, AF.Ln, [r], [r], scale=1.0 / d, bias=EPS)
            act(ap, ap, AF.Exp, [r], [r], scale=-0.5)


        dma("sync", ident[:], ident_in, P.new_sem(), [], [rconst])
        dma("sync", sel[:], sel_in, P.new_sem(), [], [rconst])
        dma("sync", cc[:], cc_in, P.new_sem(), [], [rconst])
        dma("sync", small[:], small_in.rearrange("l n -> (l n)").partition_broadcast(128), P.new_sem(), [], [rconst])
        dma("sync", bmT[:], bmT_in.rearrange("l p k -> p l k"), P.new_sem(), [], [rconst])
        dma("sync", gn[:], gn_in.rearrange("l p k -> p l k"), P.new_sem(), [], [rconst])
        dma("sync", lamt[:], lam_in.rearrange("l n -> (l n)").partition_broadcast(128), P.new_sem(), [], [rconst])
        dma("sync", gsub[:], gsub_in, P.new_sem(), [], [rconst])
        mset(ones64[:], 1.0, [rconst])

        act(silc[:], cc[:], AF.Exp, [rconst], [rmod], scale=-1.0)
        ts(silc[:], silc[:], 1.0, ALU.add, [rmod], [rmod])
        rcp(silc[:], silc[:], [rmod], [rmod])
        tt(silc[:], silc[:], cc[:], ALU.mult, [rmod, rconst], [rmod])

        def mod_pieces(l, n_lo, n_hi, feat, rows):
            for pc in range(n_lo * 2, n_hi * 2):
                stg, rstg, sstg = cur["STG"].next()
                sv = stg[:, 0:1024].rearrange("p (k n) -> p k n", k=8)
                c0 = pc * 128
                dma("sync", sv, wmod_in[l, :, c0:c0 + 128].rearrange("(k p) n -> p k n", p=128), sstg, [], [rstg])
                if feat:
                    for kc in range(8):
                        mm(U[0][:, pc * 2:pc * 2 + 2], sv[:, kc, :], silc[:, kc * 2:kc * 2 + 2], kc == 0, kc == 7,
                           [rstg, rmod], [rU[0]])
                if rows:
                    r0 = c0 - 2048
                    for kc in range(8):
                        mm(S[0][0:2, r0:r0 + 128], silc[:, kc * 2:kc * 2 + 2], sv[:, kc, :], kc == 0, kc == 7,
                           [rstg, rmod], [rS[0]])

        for l in range(n_layers):
            mod_pieces(l, 0, 12, True, False)
            tt(modT[:, l, :, :], U[0][:, 0:48].rearrange("p (k s) -> p k s", s=2),
               bmT[:, l, :].unsqueeze(2).to_broadcast([128, 24, 2]), ALU.add, [rU[0], rconst], [rmod])
            for s_ in range(2):
                ts(gmod[:, l, s_, :], modT[:, l, 8:16, s_], 1.0, ALU.add, [rmod], [rmod])
                tt(gmod[:, l, s_, :], gmod[:, l, s_, :], gn[:, l, :], ALU.mult, [rmod, rconst], [rmod])
                cp(shf[:, l, s_, :], modT[:, l, 0:8, s_], [rmod], [rmod])
            lam_init = 0.8 - 0.6 * math.exp(-0.3 * l)
            lv = lamt[:, l, :].rearrange("p (a b d) -> p a b d", a=2, b=2)
            sqv = cur["sq"][:, 0:64].rearrange("p (a d) -> p a d", a=2)
            tt(sqv, lv[:, :, 0, :], lv[:, :, 1, :], ALU.mult, [rconst], [rsq])
            red(lamw[:, l, 0:2], sqv, [rsq], [rmod])
            act(lamw[:, l, 0:2], lamw[:, l, 0:2], AF.Exp, [rmod], [rmod])
            tt(lamw[:, l, 2:3], lamw[:, l, 0:1], lamw[:, l, 1:2], ALU.subtract, [rmod], [rmod])
            ts(lamw[:, l, 3:4], lamw[:, l, 2:3], -1.0, ALU.mult, [rmod], [rmod], s2=-lam_init, op1=ALU.add)
            ts(lamw[:, l, 4:5], gsub[:, l:l + 1], 1.0 - lam_init, ALU.mult, [rconst], [rmod])

        def load_cast(dst3, src2, ncols, wres, nparts=128, kdim=8, extra_r=()):
            step = 1024 // kdim
            for c0 in range(0, ncols, step):
                cw = min(step, ncols - c0)
                stg, rstg, sstg = cur["STG"].next()
                sv = stg[0:nparts, 0:kdim * cw].rearrange("p (k n) -> p k n", k=kdim)
                dma("sync", sv, src2[:, c0:c0 + cw].rearrange("(k p) n -> p k n", p=nparts), sstg, [], [rstg])
                cp(dst3[:, :, c0:c0 + cw], sv, [rstg] + list(extra_r), wres)

        def norm_rope(ps_ap, width, groups, r_ps, rp, rrp, rope):
            sq, kn, kr, rt1, rt2 = cur["sq"], cur["kn"], cur["kr"], cur["rt1"], cur["rt2"]
            act(sq[:, 0:width], ps_ap, AF.Square, [r_ps], [rsq])
            so = 0
            for (c0, H, d, goff, ro) in groups:
                red(ss[:, so:so + H], sq[:, c0:c0 + H * d].rearrange("p (h d) -> p h d", h=H), [rsq], [rss])
                act(ss[:, so:so + H], ss[:, so:so + H], AF.Ln, [rss], [rss], scale=1.0 / d, bias=EPS)
                so += H
            act(ss[:, 0:so], ss[:, 0:so], AF.Exp, [rss], [rss], scale=-0.5)
            so = 0
            for (c0, H, d, goff, ro) in groups:
                v = kn[:, c0:c0 + H * d].rearrange("p (h d) -> p h d", h=H)
                tt(v, ps_ap[:, c0:c0 + H * d].rearrange("p (h d) -> p h d", h=H),
                   ss[:, so:so + H].unsqueeze(2).to_broadcast([128, H, d]), ALU.mult, [r_ps, rss], [rkn])
                tt(v, v, small[:, goff:goff + d].unsqueeze(1).to_broadcast([128, H, d]), ALU.mult,
                   [rkn, rconst], [rkn])
                so += H
            if not rope:
                cp(kr[:, 0:width], kn[:, 0:width], [rkn], [rkr])
                return
            for (c0, H, d, goff, ro) in groups:
                hd = d // 2
                v = kn[:, c0:c0 + H * d].rearrange("p (h i two) -> p h i two", h=H, two=2)
                o = kr[:, c0:c0 + H * d].rearrange("p (h i two) -> p h i two", h=H, two=2)
                x0, x1 = v[:, :, :, 0], v[:, :, :, 1]
                cos = rp[:, ro:ro + hd].unsqueeze(1).to_broadcast([128, H, hd])
                sin = rp[:, ro + hd:ro + 2 * hd].unsqueeze(1).to_broadcast([128, H, hd])
                a1 = rt1[:, 0:H * hd].rearrange("p (h i) -> p h i", h=H)
                a2 = rt2[:, 0:H * hd].rearrange("p (h i) -> p h i", h=H)
                tt(a1, x0, cos, ALU.mult, [rkn, rrp], [rrt])
                tt(a2, x1, sin, ALU.mult, [rkn, rrp], [rrt])
                tt(o[:, :, :, 0], a1, a2, ALU.subtract, [rrt], [rkr])
                tt(a1, x0, sin, ALU.mult, [rkn, rrp], [rrt])
                tt(a2, x1, cos, ALU.mult, [rkn, rrp], [rrt])
                tt(o[:, :, :, 1], a1, a2, ALU.add, [rrt], [rkr])

        r_hn = [Res(f"hn{i}") for i in range(NT)]
        r_loc = [[Res(f"loc{i}_{k}") for k in range(7)] for i in range(NT_OWN)]
        r_g = Res("kv_g")
        r_kx = [[Res(f"kx{i}_{k}") for k in range(3)] for i in range(2)]
        r_kx_all = [r for rr in r_kx for r in rr]
        r_loc_all = [r for rr in r_loc for r in rr]
        r_ktl = [Res(f"ktl{i}") for i in range(9)]
        r_val = [Res(f"val{i}") for i in range(66)]
        rAF = Res("arena_fence")
        ARENA_ALL = ARENA_ALL + r_ktl + r_val

        def arena_fence():
            P.op("vector", lambda e: e.memset(fence_t[:, 0:4], 0.0), (), ARENA_ALL + [rAF])
        r_o = [[Res(f"o{h}_{c}") for c in range(17)] for h in range(12)]
        r_x1 = [Res(f"x1_{i}") for i in range(NT)]
        s_cc = P.new_sem("s_cc")
        s_k = P.new_sem("s_kload")
        s_v = P.new_sem("s_vload")
        out_entries = []

        def x_src(l, i):
            if i < NT_OWN:
                base = x_in if l == 0 else x1_d
                return base[i * 128:(i + 1) * 128, :]
            base = ctx_in if l == 0 else ctx1_d
            return base[(i - NT_OWN) * 128:(i - NT_OWN + 1) * 128, :]

        def x_dst(l, i):
            if i < NT_OWN:
                base = x1_d if l < n_layers - 1 else out_d
                return base[i * 128:(i + 1) * 128, :]
            return ctx1_d[(i - NT_OWN) * 128:(i - NT_OWN + 1) * 128, :]

        def main_body():
          for l in range(n_layers):
              last = (l == n_layers - 1)
              so_ = l * NS
              if l > 0:
                  set_phase("A")
              mod_pieces(l, 8, 12, False, True)
              res32 = cur["res32"]
              dma("sync", res32[0:2, :], bmod_in[l, 2048:3072].partition_broadcast(2), P.new_sem(), [], [rres])
              for hh in range(2):
                  tt(res32[0:2, hh * 512:(hh + 1) * 512], S[0][0:2, hh * 512:(hh + 1) * 512],
                     res32[0:2, hh * 512:(hh + 1) * 512], ALU.add, [rS[0], rres], [rres])
              for s_ in range(2):
                  for hh in range(2):
                      mm(U[1][:, 0:512], sel[0:2, s_, :], res32[0:2, hh * 512:(hh + 1) * 512], True, True,
                         [rres, rconst], [rU[1]])
                      cp(gate_bc[:, s_, hh * 512:(hh + 1) * 512], U[1][:, 0:512], [rU[1]], [rgate])

              if stop == 0:
                  raise _Stop
              arena_fence()
              load_cast(wKV, win_in[l, :, C_KV:C_KV + 768], 768, [rWKV], extra_r=[rAF])
              for i in range(NT):
                  s_ = 0 if i < NT_OWN else 1
                  xt, rxt, sxt = cur["X"].next()
                  dma("sync", xt, x_src(l, i), sxt, [r_x1[i]] if l > 0 else [], [rxt])
                  if s_ == 0:
                      rp, rrp, srp = cur["RP"].next()
                      dma("sync", rp, rope_in[i], srp, [], [rrp])
                  else:
                      rp, rrp = None, None
                  sq = cur["sq"]
                  act(sq[:, 0:D], xt, AF.Square, [rxt], [rsq])
                  red(ssx[:, 0:1], sq[:, 0:D], [rsq], [rssx])
                  rstd_inplace(ssx[:, 0:1], D, rssx)
                  xn, rxn, _ = XN.next()
                  act(xn, xt, AF.Identity, [rxt, rssx], [rxn], scale=ssx[:, 0:1])
                  if stop == 0.1:
                      raise _Stop
                  for fc in range(8):
                      tr(Ubf[0][:, fc * 128:(fc + 1) * 128], xn[:, fc * 128:(fc + 1) * 128], [rxn], [rU[0]])
                  hT, rhT, shT = HT.next()
                  tt(hT, Ubf[0][:, 0:1024].rearrange("p (f t) -> p f t", f=8),
                     gmod[:, l, s_, :].unsqueeze(2).to_broadcast([128, 8, 128]), ALU.mult, [rU[0], rmod], [rhT])
                  tt(hT, hT, shf[:, l, s_, :].unsqueeze(2).to_broadcast([128, 8, 128]), ALU.add, [rhT, rmod], [rhT])
                  dma("gpsimd", hn_d[i], hT, shT, [rhT], [r_hn[i]])
                  if stop == 0.2:
                      raise _Stop
                  for (c0, c1) in ((0, 512), (512, 768)):
                      for fc in range(8):
                          mm(S[0][:, c0:c1], hT[:, fc, :], wKV[:, fc, c0:c1], fc == 0, fc == 7, [rhT, rWKV], [rS[0]])
                  groups = [(0, 2, 64, so_ + S_GKA, 0), (128, 8, 32, so_ + S_GKC, 64)]
                  norm_rope(S[0][:, 0:384], 384, groups, rS[0], rp, rrp, rope=(s_ == 0))
                  if stop == 0.3:
                      raise _Stop
                  kr = cur["kr"]
                  vb, rvb, svb = VB.next()
                  act(vb[:, 0:128], S[0][:, 384:512], AF.Copy, [rS[0]], [rvb])
                  act(vb[:, 128:384], S[0][:, 512:768], AF.Copy, [rS[0]], [rvb])
                  for j in range(3):
                      tr(Ubf[1][:, j * 128:(j + 1) * 128], kr[:, j * 128:(j + 1) * 128], [rkr], [rU[1]])
                  kts, rkts, skts = KTS.next()
                  cp(kts, Ubf[1][:, 0:384].rearrange("p (j t) -> p j t", j=3), [rU[1]], [rkts])
                  if s_ == 0:
                      cols = slice(i * 128, (i + 1) * 128)
                      for j3 in range(3):
                          for half in range(2):
                              dma("gpsimd", lock_t[j3 * 2 + half].ap()[:, cols], kts[half * 64:(half + 1) * 64, j3, :],
                                  skts, [rkts], [r_loc[i][j3 * 2 + half]])
                      dma("gpsimd", vview(locv_t[i // 4].ap())[(i % 4) * 128:(i % 4 + 1) * 128, :], vb, svb, [rvb],
                          [r_loc[i][6]])
                  else:
                      ci = i - NT_OWN
                      cols = slice(ci * 128, (ci + 1) * 128)
                      dma("gpsimd", kx_d[0:128, cols], kts[:, 0, :], skts, [rkts], [r_kx[ci][0]])
                      dma("gpsimd", kx_d[128:384, cols].rearrange("(c p) t -> p c t", p=128), kts[:, 1:3, :], skts,
                          [rkts], [r_kx[ci][1]])
                      dma("gpsimd", vx_d[ci * 128:(ci + 1) * 128, :], vb, svb, [rvb], [r_kx[ci][2]])
                  if stop is not None and 0.4 <= stop < 0.5 and i == round((stop - 0.4) * 1000):
                      raise _Stop
              if stop == 1:
                  raise _Stop
              def allgather(src_t, dst_t):
                  P.dma("gpsimd", lambda e: e.collective_compute("AllGather", ALU.bypass,
                                                                 replica_groups=[[0, 1, 2, 3], [4, 5, 6, 7]],
                                                                 ins=[src_t.ap().opt()], outs=[dst_t.ap().opt()]),
                        s_cc, reads=r_loc_all, writes=[r_g], inc=1)

              for c_ in range(6):
                  allgather(lock_t[c_], gk_t[c_])
              for c_ in range(8):
                  allgather(locv_t[c_], gv_t[c_])

              if stop == 2:
                  raise _Stop
              chunks = [(c * 4, 4, False, c * 512, [(2 * p_, 2 * p_ + 1) for p_ in range(65)]) for c in range(8)]
              if not last:
                  chunks.append((NT_OWN, 2, True, OWN, [(128, 129)]))

              set_phase("T")
              for pname in ("A", "C0", "C1"):
                  if stop == 3 and pname == "C0":
                      raise _Stop
                  if pname == "A":
                      W, qc0, krow, vcol = 512, C_QA, 0, 0
                      groups = [(0, 8, 64, so_ + S_GQA, 0)]
                      streams = [(hk * 64, 64, g, hk, hk * 4 + g) for g in range(4) for hk in range(2)]
                      scale = 64 ** -0.5
                  else:
                      ci = int(pname[1])
                      W, qc0, krow, vcol = 128, C_QC + ci * 128, 128 + ci * 128, 128 + ci * 128
                      groups = [(0, 4, 32, so_ + S_GQC, 64)]
                      streams = [((hl * 2 + m) * 32, 32, 0, hl, 8 + ci * 2 + hl) for hl in range(2) for m in range(2)]
                      scale = 32 ** -0.5
                  nj = W // 128
                  load_cast(wS[:, :, 0:W], win_in[l, :, qc0:qc0 + W], W, [rWS])
                  arena_fence()
                  if pname == "A":
                      for k0 in range(0, NKT, 26):
                          mset(VA[:, k0:k0 + 26, :, 64:128], 1.0, r_val)
                  kc0 = krow // 64
                  for r_ in range(4):
                      for half in range(2):
                          dma("gpsimd", KT[half * 64:(half + 1) * 64, r_ * OWN:(r_ + 1) * OWN],
                              gk_t[kc0 + half].ap()[r_ * 64:(r_ + 1) * 64, :], s_k, [r_g, rAF], [r_ktl[r_ * 2 + half]])
                  dma("gpsimd", KT[:, SEQ:SEQ + CTX], kx_d[krow:krow + 128, :], s_k, r_kx_all + [rAF], [r_ktl[8]])
                  for r_ in range(4):
                      for c_ in range(8):
                          vsrc = vview(gv_t[c_].ap()[r_ * 48:(r_ + 1) * 48, :])
                          for h_ in range(2):
                              dma("gpsimd", VA[:, r_ * 32 + c_ * 4:r_ * 32 + c_ * 4 + 4, h_, 0:64],
                                  vsrc[:, vcol + h_ * 64:vcol + (h_ + 1) * 64].rearrange("(i p) d -> p i d", p=128),
                                  s_v, [r_g, rAF], [r_val[(r_ * 8 + c_) * 2 + h_]])
                  for h_ in range(2):
                      dma("gpsimd", VA[:, 128:130, h_, 0:64],
                          vx_d[:, vcol + h_ * 64:vcol + (h_ + 1) * 64].rearrange("(i p) d -> p i d", p=128), s_v,
                          r_kx_all + [rAF], [r_val[64 + h_]])

                  acc_i = 0
                  s_i = 0
                  def prep_a(chunk, ti):
                      t0_, ntl_, is_ctx_ = chunk[0], chunk[1], chunk[2]
                      hn1, rhn1, shn1 = cur["HN"].next()
                      dma("sync", hn1, hn_d[t0_ + ti], shn1, [r_hn[t0_ + ti]], [rhn1])
                      if not is_ctx_:
                          rp, rrp, srp = cur["RP"].next()
                          dma("sync", rp, rope_in[t0_ + ti], srp, [], [rrp])
                      else:
                          rp, rrp = None, None
                      for fc in range(8):
                          mm(U[0][:, 0:W], hn1[:, fc, :], wS[:, fc, 0:W], fc == 0, fc == 7, [rhn1, rWS], [rU[0]])
                      norm_rope(U[0][:, 0:W], W, groups, rU[0], rp, rrp, rope=not is_ctx_)

                  def prep_b(ti, qslot):
                      qT_, rqT_ = qslot[0], qslot[1]
                      kr = cur["kr"]
                      for j in range(nj):
                          tr(Ubf[1][:, j * 128:(j + 1) * 128], kr[:, j * 128:(j + 1) * 128], [rkr], [rU[1]])
                      cp(qT_[:, 0:nj, ti * 128:(ti + 1) * 128], Ubf[1][:, 0:nj * 128].rearrange("p (j t) -> p j t", j=nj),
                         [rU[1]], [rqT_])

                  qslots = {0: QT.next()}
                  for ti in range(chunks[0][1]):
                      prep_a(chunks[0], ti)
                      prep_b(ti, qslots[0])
                  for cidx, (t0, ntl, is_ctx, ocol, pairs) in enumerate(chunks):
                      ncol = ntl * 128
                      qT, rqT = qslots[cidx][0], qslots[cidx][1]
                      nxt = chunks[cidx + 1] if cidx + 1 < len(chunks) else None
                      if nxt is not None:
                          qslots[cidx + 1] = QT.next()
                      pending = [None]

                      def after_stream(si, nxt=nxt, cidx=cidx, pending=pending):
                          if nxt is None:
                              return
                          if pending[0] is not None:
                              prep_b(pending[0], qslots[cidx + 1])
                              pending[0] = None
                          if si < nxt[1]:
                              prep_a(nxt, si)
                              pending[0] = si

                      steps = [kt for pr in pairs for kt in pr]
                      nsteps = len(steps)
                      for si in range(0, len(streams), 2):
                          spair = (streams[si], streams[si + 1])

                          def qk(step, s_base=s_i, spair=spair, steps=steps, ncol=ncol, qT=qT, rqT=rqT):
                              Sx, rSx = S[(s_base + step) % 2], rS[(s_base + step) % 2]
                              kt = steps[step]
                              for hh, (prow, kd, qj, vh, ohead) in enumerate(spair):
                                  mm(Sx[:, hh * 512:hh * 512 + ncol], KT[prow:prow + kd, kt * 128:(kt + 1) * 128],
                                     qT[prow:prow + kd, qj, 0:ncol], True, True, r_ktl + [rqT], [rSx],
                                     tp=(prow, 0) if prow == 96 else None)

                          qk(0)
                          for step in range(nsteps):
                              if step + 1 < nsteps:
                                  qk(step + 1)
                              Sx, rSx = S[(s_i + step) % 2], rS[(s_i + step) % 2]
                              kt = steps[step]
                              pt, rpt, _ = PT.next()
                              for hh in range(2):
                                  act(pt[:, hh * 512:hh * 512 + ncol], Sx[:, hh * 512:hh * 512 + ncol], AF.Exp, [rSx],
                                      [rpt], scale=scale)
                              for hh, (prow, kd, qj, vh, ohead) in enumerate(spair):
                                  mm(ACC[hh][:, 0:ncol], VA[:, kt, vh, :], pt[:, hh * 512:hh * 512 + ncol],
                                     step == 0, step == nsteps - 1, r_val + [rpt], [rACC[hh]])
                              if step == nsteps // 2 - 1:
                                  after_stream(si)
                          s_i += nsteps
                          for hh, (prow, kd, qj, vh, ohead) in enumerate(spair):
                              acc, racc = ACC[hh], rACC[hh]
                              rcp(R[64:128, 0:ncol], acc[64:128, 0:ncol], [racc], [rR])
                              if pname == "A":
                                  onb, ronb, sonb = ONB.next()
                                  tt(onb[0:64, 0:ncol], acc[0:64, 0:ncol], R[64:128, 0:ncol], ALU.mult, [racc, rR], [ronb])
                                  dma("gpsimd", o_d[:, ohead, ocol:ocol + ncol], onb[0:64, 0:ncol], sonb, [ronb],
                                      [r_o[ohead][ocol // 256], r_o[ohead][ocol // 256 + (1 if ncol == 512 else 0)]])
                              else:
                                  tt(on32[0:64, hh, 0:ncol], acc[0:64, 0:ncol], R[64:128, 0:ncol], ALU.mult, [racc, rR],
                                     [ron[hh]])
                          if pname != "A":
                              ohead = spair[0][4]
                              stt(dif[0:64, 0:ncol], on32[0:64, 1, 0:ncol], lamw[0:64, l, 3:4], on32[0:64, 0, 0:ncol],
                                  ALU.mult, ALU.add, [ron[0], ron[1], rmod], [rdif])
                              tt(dsq[0:64, 0:ncol], dif[0:64, 0:ncol], dif[0:64, 0:ncol], ALU.mult, [rdif], [rdsq])
                              mm(U[0][0:64, 0:ncol], ones64[:, :], dsq[0:64, 0:ncol], True, True, [rdsq, rconst], [rU[0]])
                              act(R[0:64, 0:ncol], U[0][0:64, 0:ncol], AF.Ln, [rU[0]], [rR], scale=1.0 / 64, bias=EPS)
                              act(R[0:64, 0:ncol], R[0:64, 0:ncol], AF.Exp, [rR], [rR], scale=-0.5)
                              tt(dif[0:64, 0:ncol], dif[0:64, 0:ncol], R[0:64, 0:ncol], ALU.mult, [rdif, rR], [rdif])
                              onb, ronb, sonb = ONB.next()
                              ts(onb[0:64, 0:ncol], dif[0:64, 0:ncol], lamw[0:64, l, 4:5], ALU.mult, [rdif, rmod], [ronb])
                              dma("gpsimd", o_d[:, ohead, ocol:ocol + ncol], onb[0:64, 0:ncol], sonb, [ronb],
                                  [r_o[ohead][ocol // 256], r_o[ohead][ocol // 256 + (1 if ncol == 512 else 0)]])
                          after_stream(si + 1)
                      if nxt is not None and pending[0] is not None:
                          prep_b(pending[0], qslots[cidx + 1])

              if stop == 4:
                  raise _Stop
              set_phase("M")
              arena_fence()
              load_cast(wM, win_in[l, :, C_M:C_M + 4352], 4352, [rWM], extra_r=[rAF])
              load_cast(wBV, win_in[l, :, C_BV:C_BV + 256], 256, [rWM], extra_r=[rAF])
              load_cast(wOUT, wout_in[l], 1024, [rWM], extra_r=[rAF])
              load_cast(wPA, wpa_in[l], 1024, [rWM], extra_r=[rAF], kdim=4)
              load_cast(wPC, wpc_in[l], 1024, [rWM], extra_r=[rAF], kdim=2)
              load_cast(wPB, wpb_in[l], 1024, [rWM], extra_r=[rAF], nparts=64, kdim=4)
              stg, rstg, sstg = cur["STG"].next()
              sv = stg[:, 0:512].rearrange("p (g q) -> p g q", g=4)
              dma("sync", sv, wsp_in[l], sstg, [], [rstg])
              cp(wsp, sv, [rstg], [rwsp])
              bsp_bc = small[0:64, so_ + S_BSP:so_ + S_BSP + 512].rearrange("p (g t) -> p g t", g=4)
              gsgu_bc = small[:, so_ + S_SGU:so_ + S_SGU + 256]
              sq, kn, res32 = cur["sq"], cur["kn"], cur["res32"]

              def sigmoid_from(ps_ap, r_ps, eb, rebx):
                  act(eb, ps_ap, AF.Exp, [r_ps], [rebx], scale=-1.0)
                  ts(eb, eb, 1.0, ALU.add, [rebx], [rebx])
                  rcp(eb, eb, [rebx], [rebx])

              mchunks = [(h * 2, 2, False, h * 256) for h in range(16)]
              if not last:
                  mchunks.append((NT_OWN, 2, True, OWN))
              for (t0, ntl, is_ctx, ocol) in mchunks:
                  ncol = NM
                  s_ = 1 if is_ctx else 0
                  hn, rhn, shn = cur["HN"].next()
                  dma("sync", hn, hn_d[t0:t0 + ntl].rearrange("i p f t -> p i f t"), shn, r_hn[t0:t0 + ntl], [rhn])
                  og, rog, sog = OG.next()
                  for par in range(2):
                      dma("sync", og[par * 64:(par + 1) * 64, :, :],
                          o_d[:, :, ocol:ocol + ncol].rearrange("d (c two) t -> d two c t", two=2)[:, par],
                          sog, [r_o[h][ocol // 256] for h in range(12)], [rog])

                  def proj(ps_ap, col0, ncols_w, r_ps):
                      for fc in range(8):
                          mm(ps_ap, wM[:, fc, col0:col0 + ncols_w], hn[:, :, fc, :], fc == 0, fc == 7, [rWM, rhn], [r_ps])

                  for ti in range(ntl):
                      for fc in range(8):
                          mm(U[0][:, 0:256], hn[:, ti, fc, :], wBV[:, fc, :], fc == 0, fc == 7, [rhn, rWM], [rU[0]])
                      act(sq[:, 0:256], U[0][:, 0:256], AF.Square, [rU[0]], [rsq])
                      red(ss[:, 0:4], sq[:, 0:256].rearrange("p (h d) -> p h d", h=4), [rsq], [rss])
                      rstd_inplace(ss[:, 0:4], 64, rss)
                      tt(kn[:, 0:256].rearrange("p (h d) -> p h d", h=4), U[0][:, 0:256].rearrange("p (h d) -> p h d", h=4),
                         ss[:, 0:4].unsqueeze(2).to_broadcast([128, 4, 64]), ALU.mult, [rU[0], rss], [rkn])
                      tt(vn, kn[:, 0:256], gsgu_bc, ALU.mult, [rkn, rconst], [rvn])
                      for g in range(4):
                          mm(U[1][0:64, g * 128:(g + 1) * 128], vn[:, g * 64:(g + 1) * 64], wsp[:, g, :], True, True,
                             [rvn, rwsp], [rU[1]])
                      tt(mixsb[0:64, :, ti * 128:(ti + 1) * 128], U[1][0:64, 0:512].rearrange("p (g t) -> p g t", g=4),
                         bsp_bc, ALU.add, [rU[1], rconst], [rmix])

                  zi = 0
                  for blk in range(6):
                      zps, rz = ACC[zi % 2], rACC[zi % 2]
                      col0 = blk * 128 if blk < 4 else 768 + (blk - 4) * 128
                      proj(zps[:, 0:ncol], col0, 128, rz)
                      eb, tb = ebuf[:, zi % 2, :], tbuf[:, zi % 2, :]
                      sigmoid_from(zps[:, 0:ncol], rz, eb, reb[zi % 2])
                      tt(tb, zps[:, 0:ncol], eb, ALU.mult, [rz, reb[zi % 2]], [rtb[zi % 2]])
                      tt(G_AC[:, blk, :], tb, og[:, blk, :], ALU.mult, [rtb[zi % 2], rog], [rG])
                      zi += 1
                  for g in range(4):
                      zps, rz = ACC[zi % 2], rACC[zi % 2]
                      proj(zps[0:64, 0:ncol], 512 + g * 64, 64, rz)
                      eb, tb = ebuf[0:64, zi % 2, :], tbuf[0:64, zi % 2, :]
                      sigmoid_from(zps[0:64, 0:ncol], rz, eb, reb[zi % 2])
                      tt(tb, zps[0:64, 0:ncol], eb, ALU.mult, [rz, reb[zi % 2]], [rtb[zi % 2]])
                      ups, ru = U[g % 2], rU[g % 2]
                      proj(ups[0:64, 0:ncol], 1024 + g * 64, 64, ru)
                      tt(eb, ups[0:64, 0:ncol], mixsb[0:64, g, :], ALU.mult, [ru, rmix], [reb[zi % 2]])
                      tt(G_B[0:64, g, :], tb, eb, ALU.mult, [rtb[zi % 2], reb[zi % 2]], [rG])
                      zi += 1

                  for n in range(8):
                      nb = slice(n * 128, (n + 1) * 128)
                      ysl = [(S[0][:, 0:ncol], rS[0]), (S[0][:, 512:512 + ncol], rS[0]), (S[1][:, 0:ncol], rS[1])]
                      for c_ in range(4):
                          mm(ysl[0][0], wPA[:, c_, nb], G_AC[:, c_, :], c_ == 0, c_ == 3, [rWM, rG], [ysl[0][1]])
                      for g in range(4):
                          mm(ysl[1][0], wPB[0:64, g, nb], G_B[0:64, g, :], g == 0, g == 3, [rWM, rG], [ysl[1][1]])
                      for c_ in range(2):
                          mm(ysl[2][0], wPC[:, c_, nb], G_AC[:, 4 + c_, :], c_ == 0, c_ == 1, [rWM, rG], [ysl[2][1]])
                      for br in range(3):
                          gi = (n * 3 + br) % 2
                          gps, rg_ = ACC[gi], rACC[gi]
                          proj(gps[:, 0:ncol], 1280 + br * 1024 + n * 128, 128, rg_)
                          sigmoid_from(gps[:, 0:ncol], rg_, sig[:, br, :], rsig[br])
                      tt(macc, ysl[0][0], sig[:, 0, :], ALU.mult, [ysl[0][1], rsig[0]], [rmacc])
                      tt(tbuf[:, 0, :], ysl[1][0], sig[:, 1, :], ALU.mult, [ysl[1][1], rsig[1]], [rtb[0]])
                      tt(macc, macc, tbuf[:, 0, :], ALU.add, [rmacc, rtb[0]], [rmacc])
                      tt(tbuf[:, 1, :], ysl[2][0], sig[:, 2, :], ALU.mult, [ysl[2][1], rsig[2]], [rtb[1]])
                      tt(mT[:, n, :], macc, tbuf[:, 1, :], ALU.add, [rmacc, rtb[1]], [rmT])

                  for ti in range(ntl):
                      i = t0 + ti
                      xt, rxt, sxt = cur["X"].next()
                      dma("sync", xt, x_src(l, i), sxt, [r_x1[i]] if l > 0 else [], [rxt])
                      for hh in range(2):
                          for k_ in range(8):
                              mm(S[1][:, hh * 512:(hh + 1) * 512], mT[:, k_, ti * 128:(ti + 1) * 128],
                                 wOUT[:, k_, hh * 512:(hh + 1) * 512], k_ == 0, k_ == 7, [rmT, rWM], [rS[1]])
                      for hh in range(2):
                          tt(res32[:, hh * 512:(hh + 1) * 512], S[1][:, hh * 512:(hh + 1) * 512],
                             gate_bc[:, s_, hh * 512:(hh + 1) * 512], ALU.mult, [rS[1], rgate], [rres])
                      tt(xt, xt, res32, ALU.add, [rxt, rres], [rxt])
                      e = dma("gpsimd", x_dst(l, i), xt, sxt, [rxt], [r_x1[i]])
                      if last:
                          out_entries.append(e)


        try:
            main_body()
        except _Stop:
            pass
        if not out_entries:
            out_entries.append(dma("sync", out_d[0:128, 0:16], ss[:, 0:16], P.new_sem(), [rss], []))
        for e in out_entries:
            P.wait_at_end(e)
        stats = P.emit()
    return nc, stats


def _rope_tables(j):
    n = j * OWN + np.arange(OWN)
    rows = (n // 64).astype(np.float32)
    cols = (n % 64).astype(np.float32)
    out = np.zeros((OWN, 96), np.float32)

    def tab(dim):
        quarter = dim // 4
        inv = np.power(np.float32(10000.0), -np.arange(quarter, dtype=np.float32) / quarter).astype(np.float32)
        ang = np.concatenate([rows[:, None] * inv, cols[:, None] * inv], axis=-1).astype(np.float32)
        return np.cos(ang).astype(np.float32), np.sin(ang).astype(np.float32)

    ca, sa = tab(64)
    cq, sq_ = tab(32)
    out[:, 0:32], out[:, 32:64], out[:, 64:80], out[:, 80:96] = ca, sa, cq, sq_
    return np.ascontiguousarray(out.reshape(NT_OWN, 128, 96))


def _perm_cols():
    r = np.arange
    qa = np.array([(hk * 4 + g) * 64 + d for g in range(4) for hk in range(2) for d in range(64)])
    return np.concatenate([r(768, 896), r(1024, 1280), r(896, 1024), r(1280, 1536), qa, r(512, 768),
                           r(1536, 2048), r(2816, 3072), r(2048, 2304), r(2304, 2560), r(3072, 6144), r(2560, 2816)])


_CACHE = {}


def kernel(x, c, ctx, c_ctx, w_mod, b_mod, g_norm, w_in, gq_a, gk_a, gq_c, gk_c,
           g_sgu, w_sp, b_sp, lam_c, g_subln, w_pa, w_pb, w_pc, w_out):
    f = lambda a: np.ascontiguousarray(np.asarray(a, dtype=np.float32))
    x, c, ctx, c_ctx = f(x), f(c), f(ctx), f(c_ctx)
    w_mod, b_mod, g_norm, w_in = f(w_mod), f(b_mod), f(g_norm), f(w_in)
    L = w_in.shape[0]
    if "nc" not in _CACHE:
        _CACHE["nc"] = build_program(L)
    nc, _ = _CACHE["nc"]
    w_in_p = np.ascontiguousarray(w_in[:, :, _perm_cols()])
    small = np.concatenate([f(gq_a), f(gk_a), f(gq_c), f(gk_c), f(g_sgu).reshape(L, 256),
                            f(b_sp).reshape(L, 512)], axis=1)
    shared = {
        "w_mod": w_mod,
        "bmT": np.ascontiguousarray(b_mod.reshape(L, 24, 128).transpose(0, 2, 1)),
        "b_mod": b_mod,
        "gn": np.ascontiguousarray(g_norm.reshape(L, 8, 128).transpose(0, 2, 1)),
        "w_in": w_in_p,
        "small": np.ascontiguousarray(small),
        "gsub": np.ascontiguousarray(np.tile(f(g_subln), (1, 2)).T),
        "lam": np.ascontiguousarray(f(lam_c).reshape(L, 128)),
        "w_spT": np.ascontiguousarray(f(w_sp).transpose(0, 3, 1, 2)),
        "w_pa": f(w_pa),
        "w_pb": f(w_pb),
        "w_pc": f(w_pc),
        "w_out": f(w_out),
        "ident": np.eye(128, dtype=np.float32).astype(ml_dtypes.bfloat16),
        "sel": np.ascontiguousarray(np.stack([np.stack([np.ones(128), np.zeros(128)]),
                                              np.stack([np.zeros(128), np.ones(128)])], axis=1).astype(np.float32)),
    }
    in_maps = []
    for core in range(NCORES):
        b, j = core // 4, core % 4
        cc2 = np.stack([c[b], c_ctx], axis=-1).reshape(8, 128, 2).transpose(1, 0, 2).reshape(128, 16)
        m = dict(shared)
        m["x"] = np.ascontiguousarray(x[b, j * OWN:(j + 1) * OWN])
        m["ctx"] = np.ascontiguousarray(ctx[b])
        m["cc"] = np.ascontiguousarray(cc2)
        m["rope"] = _rope_tables(j)
        in_maps.append(m)
    res = run_bass_kernel_spmd(nc, in_maps, core_ids=list(range(NCORES)))
    out = np.empty((2, SEQ, D), np.float32)
    for core in range(NCORES):
        b, j = core // 4, core % 4
        out[b, j * OWN:(j + 1) * OWN] = res.results[core]["out"]
    return out
```
